# Optimizing a Trainium2 kernel written in Bass

```python
import math
import jax, jax.numpy as jnp
from jax import lax
import numpy as np

D_MODEL = 2048
BATCH = 1
SEQ = 8192
DEPTH = 1

POOL_WINDOWS = (2, 4, 8, 16)
POOL_GROUPS = len(POOL_WINDOWS)
POOL_GROUP_WIDTH = D_MODEL // 8
POOL_WIDTH = POOL_GROUPS * POOL_GROUP_WIDTH

ATTN_GROUPS = ((128, 1), (512, 4), (2048, 16))
HEADS_PER_GROUP = 4
N_ATTN_HEADS = HEADS_PER_GROUP * len(ATTN_GROUPS)
HEAD_DIM = 128
ATTN_WIDTH = N_ATTN_HEADS * HEAD_DIM
ATTN_OUT_WIDTH = HEADS_PER_GROUP * HEAD_DIM

N_BRANCHES = 2
IN_WIDTH = POOL_WIDTH + 3 * ATTN_WIDTH + N_BRANCHES * D_MODEL

D_FF = 5632
CONV_WIDTH = 3

RMS_EPS = 1e-6

kernel_name = "hybrid_pool_dilated_alibi_convffn_block"


def alibi_slopes(n_heads):
    return np.array([2.0 ** (-8.0 * (h + 1) / n_heads) for h in range(n_heads)], dtype=np.float32)


def rms_norm(x, g):
    xf = x.astype(jnp.float32)
    y = xf * lax.rsqrt(jnp.mean(xf * xf, axis=-1, keepdims=True) + RMS_EPS) * g.astype(jnp.float32)
    return y.astype(x.dtype)


def pool_mixer(u, w_lin, scale):
    B, S, _ = u.shape
    uf = u.astype(jnp.float32).reshape(B, S, POOL_GROUPS, POOL_GROUP_WIDTH)
    t = jnp.arange(S)
    outs = []
    for gi, w in enumerate(POOL_WINDOWS):
        ug = uf[:, :, gi]
        cs = jnp.cumsum(ug, axis=1)
        lag = jnp.pad(cs, ((0, 0), (w, 0), (0, 0)))[:, :S]
        cnt = jnp.minimum(t + 1, w).astype(jnp.float32)[None, :, None]
        outs.append((cs - lag) / cnt - ug)
    pooled = jnp.stack(outs, axis=2)
    y = jnp.einsum('bsgc,gce->bsge', pooled, w_lin.astype(jnp.float32))
    y = y.reshape(B, S, POOL_WIDTH) * scale.astype(jnp.float32)
    return y.astype(u.dtype)


def dilated_group_attention(q, k, v, slopes, window, dilation):
    B, S, H, Dh = q.shape
    span = window // dilation
    L = S // dilation
    nb = -(-L // span)
    Lp = nb * span
    N = B * dilation

    def to_sub(a):
        a = a.reshape(B, L, dilation, H, Dh).transpose(0, 2, 1, 3, 4).reshape(N, L, H, Dh)
        a = jnp.pad(a, ((0, 0), (0, Lp - L), (0, 0), (0, 0)))
        return a.reshape(N, nb, span, H, Dh)

    def with_prev(a):
        prev = jnp.pad(a, ((0, 0), (1, 0), (0, 0), (0, 0), (0, 0)))[:, :nb]
        return jnp.concatenate([prev, a], axis=2)

    qb = to_sub(q)
    kk = with_prev(to_sub(k))
    vv = with_prev(to_sub(v))

    s = jnp.einsum('nbqhd,nbkhd->nbhqk', qb, kk, preferred_element_type=jnp.float32) * (Dh ** -0.5)
    qi = jnp.arange(span)[:, None] + span
    ki = jnp.arange(2 * span)[None, :]
    j = qi - ki
    key_abs = (jnp.arange(nb) * span)[:, None] - span + jnp.arange(2 * span)[None, :]
    valid = ((j >= 0) & (j <= span))[None] & (key_abs >= 0)[:, None, :]
    bias = -slopes[:, None, None] * (j * dilation).astype(jnp.float32)[None]
    s = jnp.where(valid[None, :, None], s + bias[None, None], -jnp.inf)
    m = jnp.max(s, axis=-1, keepdims=True)
    p = jnp.exp(s - m)
    l = jnp.sum(p, axis=-1, keepdims=True)
    o = jnp.einsum('nbhqk,nbkhd->nbqhd', p, vv.astype(jnp.float32)) / jnp.swapaxes(l, 2, 3)
    lse = jnp.swapaxes((m + jnp.log(l))[..., 0], 2, 3)

    o = o.reshape(N, Lp, H, Dh)[:, :L].reshape(B, dilation, L, H, Dh).transpose(0, 2, 1, 3, 4).reshape(B, S, H, Dh)
    lse = lse.reshape(N, Lp, H)[:, :L].reshape(B, dilation, L, H).transpose(0, 2, 1, 3).reshape(B, S, H)
    return o, lse


def dilated_attention_mixer(q, k, v):
    B, S = q.shape[:2]
    slopes = jnp.asarray(alibi_slopes(N_ATTN_HEADS))
    outs, lses = [], []
    for gi, (window, dilation) in enumerate(ATTN_GROUPS):
        hs = slice(gi * HEADS_PER_GROUP, (gi + 1) * HEADS_PER_GROUP)
        o, lse = dilated_group_attention(q[:, :, hs], k[:, :, hs], v[:, :, hs], slopes[hs], window, dilation)
        outs.append(o)
        lses.append(lse)
    wts = jax.nn.softmax(jnp.stack(lses, axis=0), axis=0)
    y = jnp.sum(wts[..., None] * jnp.stack(outs, axis=0), axis=0)
    return y.reshape(B, S, ATTN_OUT_WIDTH).astype(q.dtype)


def causal_dwconv(u, w, b):
    S = u.shape[1]
    up = jnp.pad(u, ((0, 0), (CONV_WIDTH - 1, 0), (0, 0)))
    y = b
    for i in range(CONV_WIDTH):
        y = y + w[i] * up[:, i:i + S]
    return y


def setup_inputs(seed: int = 0) -> dict:
    key = jax.random.key(seed)
    ks = jax.random.split(key, 16)
    f32 = jnp.float32
    nrm = lambda k, shape, fan: jax.random.normal(k, shape, f32) * (fan ** -0.5)
    return {
        "x": jax.random.normal(ks[0], (BATCH, SEQ, D_MODEL), f32),
        "g_mix": 1.0 + 0.02 * jax.random.normal(ks[1], (DEPTH, D_MODEL), f32),
        "w_in": nrm(ks[2], (DEPTH, D_MODEL, IN_WIDTH), D_MODEL),
        "b_gate": 0.1 * jax.random.normal(ks[3], (DEPTH, N_BRANCHES * D_MODEL), f32),
        "w_pool_lin": nrm(ks[4], (DEPTH, POOL_GROUPS, POOL_GROUP_WIDTH, POOL_GROUP_WIDTH), POOL_GROUP_WIDTH),
        "pool_scale": 1.0 + 0.02 * jax.random.normal(ks[5], (DEPTH, POOL_WIDTH), f32),
        "w_pool_out": nrm(ks[6], (DEPTH, POOL_WIDTH, D_MODEL), POOL_WIDTH),
        "w_attn_out": nrm(ks[7], (DEPTH, ATTN_OUT_WIDTH, D_MODEL), ATTN_OUT_WIDTH),
        "w_out": nrm(ks[8], (DEPTH, D_MODEL, D_MODEL), D_MODEL),
        "g_ffn": 1.0 + 0.02 * jax.random.normal(ks[9], (DEPTH, D_MODEL), f32),
        "w_up": nrm(ks[10], (DEPTH, D_MODEL, 2 * D_FF), D_MODEL),
        "conv_w": nrm(ks[11], (DEPTH, CONV_WIDTH, 2 * D_FF), CONV_WIDTH),
        "conv_b": 0.02 * jax.random.normal(ks[12], (DEPTH, 2 * D_FF), f32),
        "w_down": nrm(ks[13], (DEPTH, D_FF, D_MODEL), D_FF),
        "g_final": 1.0 + 0.02 * jax.random.normal(ks[14], (D_MODEL,), f32),
    }


def reference(x, g_mix, w_in, b_gate, w_pool_lin, pool_scale, w_pool_out, w_attn_out, w_out,
              g_ffn, w_up, conv_w, conv_b, w_down, g_final):
    B, S, _ = x.shape
    o_q = POOL_WIDTH
    o_k = o_q + ATTN_WIDTH
    o_v = o_k + ATTN_WIDTH
    o_g = o_v + ATTN_WIDTH
    for l in range(DEPTH):
        h = rms_norm(x, g_mix[l])
        proj = h @ w_in[l]
        u = proj[..., :o_q]
        q = proj[..., o_q:o_k].reshape(B, S, N_ATTN_HEADS, HEAD_DIM)
        k = proj[..., o_k:o_v].reshape(B, S, N_ATTN_HEADS, HEAD_DIM)
        v = proj[..., o_v:o_g].reshape(B, S, N_ATTN_HEADS, HEAD_DIM)
        gates = jax.nn.sigmoid(proj[..., o_g:] + b_gate[l]).reshape(B, S, N_BRANCHES, D_MODEL)

        y_pool = pool_mixer(u, w_pool_lin[l], pool_scale[l]) @ w_pool_out[l]
        y_attn = dilated_attention_mixer(q, k, v) @ w_attn_out[l]
        mixed = gates[:, :, 0] * y_pool + gates[:, :, 1] * y_attn
        x = x + mixed @ w_out[l]

        h = rms_norm(x, g_ffn[l])
        up = causal_dwconv(h @ w_up[l], conv_w[l], conv_b[l])
        a, b = up[..., :D_FF], up[..., D_FF:]
        x = x + (jax.nn.gelu(a, approximate=False) * b) @ w_down[l]
    return rms_norm(x, g_final)
```

```python
import numpy as np
import concourse.bass as bass
import concourse.mybir as mybir
from concourse.bass_utils import run_bass_kernel_spmd

F32 = mybir.dt.float32
BF16 = mybir.dt.bfloat16
AF = mybir.ActivationFunctionType
ALU = mybir.AluOpType

NCORES = 8
D = 2048
KC = 16
T = 1024
TQ = 1026
J0 = 2048
NT = J0 + TQ
UH = 15
TU = TQ + UH
OQ, OK_, OV, OG = 1024, 2560, 4096, 5632
INW = 9728
DFF = 5632
FC = 44
DIL = [1, 4, 16]
HALO = [128, 512, 2048]
POOLW = [2, 4, 8, 16]
NEG = -30000.0
EPS = 1e-6
NKV = 96
NX1 = 128 + TQ
TB_GMIX, TB_BG, TB_PS, TB_GFFN, TB_CW, TB_CB, TB_GF, TB_CORR, TB_KV = 0, 16, 48, 56, 72, 336, 424, 440, 512
NTAB = TB_KV + NKV
ARENA_WORDS = 52600
ENGS = ("pe", "act", "dve", "pool", "sp")


class Buf:
    __slots__ = ("name", "lws", "rd", "drd", "sem", "dcount", "dma", "keep")

    def __init__(self, name, dma=False):
        self.name = name
        self.keep = False
        self.lws = []
        self.rd = {}
        self.drd = []
        self.sem = None
        self.dcount = 0
        self.dma = dma


class Op:
    __slots__ = ("eng", "fn", "deps", "is_dma", "dbuf", "event", "sig")

    def __init__(self, eng, fn, is_dma, dbuf):
        self.eng = eng
        self.fn = fn
        self.deps = set()
        self.is_dma = is_dma
        self.dbuf = dbuf
        self.event = None
        self.sig = False


class Kern:
    def __init__(self):
        self.streams = {e: [] for e in ENGS}
        self.bufs = []
        self.pending = {e: [] for e in ENGS}
        self.dma_since = []
        self.last_barrier = []

    def buf(self, name, dma=False):
        b = Buf(name, dma)
        self.bufs.append(b)
        return b

    def op(self, eng, fn, reads=(), writes=(), pwrites=(), dma_buf=None, extra=()):
        o = Op(eng, fn, dma_buf is not None, dma_buf)
        deps = o.deps
        deps.update(extra)
        if self.pending[eng]:
            deps.update(self.pending[eng])
            self.pending[eng] = []
        for b in reads:
            deps.update(b.lws)
        for b in writes:
            deps.update(b.lws)
            deps.update(b.rd.values())
            deps.update(b.drd)
        for b in pwrites:
            deps.update(b.rd.values())
            deps.update(b.drd)
        if not o.is_dma and eng == "pe":
            o.deps = deps = {d for d in deps if d.is_dma or d.eng != "pe"}
        deps.discard(o)
        for b in reads:
            if o.is_dma:
                b.drd.append(o)
            else:
                b.rd[eng] = o
        for b in writes:
            b.lws = [o]
            b.rd = {}
            b.drd = []
        for b in pwrites:
            b.lws.append(o)
        self.streams[eng].append(o)
        if o.is_dma:
            self.dma_since.append(o)
        return o

    def barrier(self):
        last = [s[-1] for e, s in self.streams.items() if s and e != "pool"]
        allops = last + [o for o in self.dma_since if not o.dbuf.keep]
        for e in ENGS:
            if e != "pool":
                self.pending[e] = list(allops)
        self.last_barrier = list(allops)
        self.dma_since = [o for o in self.dma_since if o.dbuf.keep]
        for b in self.bufs:
            if b.keep:
                continue
            b.lws = []
            b.rd = {}
            b.drd = []

    def emit(self, nc):
        for ops in self.streams.values():
            for o in ops:
                for d in o.deps:
                    d.sig = True
        engsem = {e: nc.alloc_semaphore("s_" + e) for e in ENGS}
        for b in self.bufs:
            if b.dma:
                b.sem = nc.alloc_semaphore("d_" + b.name)
        for e in ENGS:
            tick = 0
            for o in self.streams[e]:
                if o.is_dma:
                    b = o.dbuf
                    b.dcount += 1
                    o.event = (b.sem, 16 * b.dcount)
                elif o.sig:
                    tick += 1
                    o.event = (engsem[e], tick)
        streams = self.streams

        def run(e, h):
            seen = {}
            for o in streams[e]:
                need = {}
                for d in o.deps:
                    s, v = d.event
                    if need.get(s, 0) < v:
                        need[s] = v
                for s, v in need.items():
                    if seen.get(s, 0) < v:
                        h.wait_ge(s, v)
                        seen[s] = v
                if o.fn is not None:
                    ins = o.fn(h)
                    if o.is_dma:
                        ins.then_inc(o.event[0], 16)
                    elif o.sig:
                        ins.then_inc(o.event[0], 1)

        with nc.Block() as block:
            @block.tensor
            def _(h):
                run("pe", h)

            @block.scalar
            def _(h):
                run("act", h)

            @block.vector
            def _(h):
                run("dve", h)

            @block.gpsimd
            def _(h):
                run("pool", h)

            @block.sync
            def _(h):
                run("sp", h)


class Arena:
    def __init__(self, ap, nwords):
        self.ap = ap
        self.lo = 0
        self.hi = nwords

    def _take(self, nbytes, top):
        nw = (nbytes + 3) // 4
        nw = (nw + 7) // 8 * 8
        if top:
            self.hi -= nw
            off = self.hi
        else:
            off = self.lo
            self.lo += nw
        assert self.lo <= self.hi, ("SBUF arena overflow", self.lo, self.hi)
        return off, nw

    def f32(self, cols, top=False):
        off, nw = self._take(cols * 4, top)
        return self.ap[:, off:off + cols]

    def bf16(self, cols, top=False):
        assert cols % 2 == 0
        off, nw = self._take(cols * 2, top)
        return self.ap[:, off:off + cols // 2].bitcast(BF16)


def split_mult(start, total, maxc, mult):
    assert total % mult == 0
    units = total // mult
    n = -(-total // (maxc // mult * mult))
    base, rem = divmod(units, n)
    out = []
    s = start
    for i in range(n):
        c = (base + (1 if i < rem else 0)) * mult
        out.append((s, c))
        s += c
    return out


def split_chunks(start, total, maxc):
    n = -(-total // maxc)
    base, rem = divmod(total, n)
    out = []
    s = start
    for i in range(n):
        c = base + (1 if i < rem else 0)
        out.append((s, c))
        s += c
    return out


def build_program():
    nc = bass.Bass("TRN2", target_bir_lowering=False)
    dr = {}

    def din(name, shape):
        dr[name] = nc.dram_tensor(name, list(shape), F32, kind="ExternalInput").ap()
        return dr[name]

    xT = din("xT", (D, NT))
    w_in = din("w_in", (19 * 128, 16 * 512))
    wpl = din("wpl", (4 * 128, 2 * 256))
    wpa = din("wpa", (4 * 128, 12 * 512))
    w_out = din("w_out", (4 * 128, 16 * 512))
    w_up = din("w_up", (FC * 128, 2 * 16 * 128))
    w_down = din("w_down", (16 * 128, 11 * 512))
    tabd = din("tab", (128, NTAB))
    masksd = din("masks", (128, 12 * 256))
    identd = din("ident", (128, 128))
    outT = nc.dram_tensor("outT", [D, T], F32, kind="ExternalOutput").ap()

    K = Kern()
    kvcols = {}

    arena_t = nc.sbuf_tensor("arena", [128, ARENA_WORDS], F32).__enter__()
    psA = nc.psum_tensor("psA", [128, 3, 512], F32).__enter__()
    psB = nc.psum_tensor("psB", [128, 3, 512], F32).__enter__()
    psC = nc.psum_tensor("psC", [128, 512], F32).__enter__()
    psD = nc.psum_tensor("psD", [128, 512], F32).__enter__()
    A = Arena(arena_t[:, :], ARENA_WORDS)

    def bank(i):
        if i < 3:
            return psA[:, i, :]
        if i < 6:
            return psB[:, i - 3, :]
        return psC[:, :] if i == 6 else psD[:, :]

    def grp(gi, nb, w):
        return (psA if gi == 0 else psB)[:, 0:nb, 0:w]

    bk = [K.buf("bank%d" % i) for i in range(8)]
    grpb = [[bk[0], bk[1], bk[2]], [bk[3], bk[4], bk[5]]]
    st = {"g": 0, "sb": 0, "w": 0, "b6": 0, "ev": 0}

    def mm(out, lhsT, rhs, start, stop, reads, writes):
        return K.op("pe", lambda e, o=out, l=lhsT, r=rhs, s=start, t=stop: e.matmul(o, lhsT=l, rhs=r, start=s, stop=t),
             reads=reads, writes=writes)

    def act(out, in_, func, reads, writes=(), pwrites=(), bias=None, scale=None):
        kw = {}
        if bias is not None:
            kw["bias"] = bias
        if scale is not None:
            kw["scale"] = scale
        K.op("act", lambda e, o=out, i=in_, f=func, kw=kw: e.activation(out=o, in_=i, func=f, **kw),
             reads=reads, writes=writes, pwrites=pwrites)

    def dve_tt(out, in0, in1, op, reads, writes=(), pwrites=(), extra=()):
        return K.op("dve", lambda e, o=out, a=in0, b=in1, p=op: e.tensor_tensor(out=o, in0=a, in1=b, op=p),
             reads=reads, writes=writes, pwrites=pwrites, extra=extra)

    def dve_stt(out, in0, scalar, in1, op0, op1, reads, writes=(), pwrites=()):
        K.op("dve", lambda e, o=out, a=in0, s=scalar, b=in1, p0=op0, p1=op1:
             e.scalar_tensor_tensor(out=o, in0=a, scalar=s, in1=b, op0=p0, op1=p1),
             reads=reads, writes=writes, pwrites=pwrites)

    def dve_copy(out, in_, reads, writes=(), pwrites=()):
        K.op("dve", lambda e, o=out, i=in_: e.tensor_copy(out=o, in_=i), reads=reads, writes=writes, pwrites=pwrites)

    def dma(eng, out, in_, buf, reads=(), writes=()):
        K.op(eng, lambda e, o=out, i=in_: e.dma_start(out=o, in_=i), reads=reads, writes=writes, dma_buf=buf)

    tab = A.f32(NTAB)
    tabb = K.buf("tab", dma=True)
    ones = A.bf16(128)
    onesb = K.buf("ones")
    ident = A.bf16(128)
    identb = K.buf("ident", dma=True)
    epsb_ap = A.f32(8)
    epsb = K.buf("eps")
    rsblk = A.f32(1040)
    rs = [rsblk[:, 0:520], rsblk[:, 520:1040]]
    rsb = [K.buf("rs0"), K.buf("rs1")]
    WSLOT = 8192
    wslot = [A.bf16(WSLOT) for _ in range(3)]
    wbuf = [K.buf("w%d" % i, dma=True) for i in range(3)]
    for wb_ in wbuf:
        wb_.keep = True
    st["nslots"] = 3

    dma("sp", tab, tabd, tabb, writes=[tabb])
    dma("pool", ident, identd, identb, writes=[identb])
    K.op("dve", lambda e: e.memset(ones, 1.0), writes=[onesb])
    K.op("dve", lambda e: e.memset(epsb_ap, EPS), writes=[epsb])

    def slot_extra(i):
        if i == 3 and not st.get("w3used"):
            st["w3used"] = True
            return list(st["w3deps"])
        return ()

    def load_flat(src2d, n):
        i = st["w"]
        st["w"] = (i + 1) % st["nslots"]
        assert n <= WSLOT
        dst = wslot[i][:, 0:n]
        K.op("pool", lambda e, o=dst, s_=src2d: e.dma_start(out=o, in_=s_), writes=[wbuf[i]], dma_buf=wbuf[i],
             extra=slot_extra(i))
        return dst, wbuf[i]

    def load_w(W, k0, nk, c0, ncols):
        if W is w_in or W is w_out:
            assert k0 == 0 and nk == 16 and ncols == 512 and c0 % 512 == 0
            b_ = c0 // 512
            dst, wb_ = load_flat(W[b_ * 128:(b_ + 1) * 128, :], 8192)
        elif W is wpl:
            assert nk == 2 and ncols == 256 and c0 == 0
            gi_ = k0 // 2
            dst, wb_ = load_flat(W[gi_ * 128:(gi_ + 1) * 128, :], 512)
        elif W is w_down:
            assert nk == 11 and ncols == 512
            rb_ = (k0 // 11) * 4 + c0 // 512
            dst, wb_ = load_flat(W[rb_ * 128:(rb_ + 1) * 128, :], 11 * 512)
        elif W is w_up:
            assert k0 == 0 and nk == 16 and ncols == 128
            half_ = 0 if c0 < DFF else 1
            f_ = (c0 - half_ * DFF) // 128
            dst, wb_ = load_flat(W[f_ * 128:(f_ + 1) * 128, half_ * 2048:(half_ + 1) * 2048], 2048)
        else:
            raise AssertionError("unknown weight")
        return dst.rearrange("p (k n) -> p k n", k=nk), wb_

    def load_w2(blk):
        dst, wb_ = load_flat(wpa[blk * 128:(blk + 1) * 128, :], 12 * 512)
        return (dst[:, 0:4096].rearrange("p (k n) -> p k n", k=8),
                dst[:, 4096:6144].rearrange("p (k n) -> p k n", k=4), wb_)

    def next_grp():
        g = st["g"]
        st["g"] = 1 - g
        return g

    def next_bank6():
        b = st["sb"]
        st["sb"] = (b + 1) % 6
        return b

    def norm_core(xap, xbufs, n, gcol, hout, hbuf, sqap, sqbuf):
        act(sqap, xap, AF.Square, reads=xbufs, pwrites=[sqbuf])
        b6 = 6 + st["b6"]
        st["b6"] = 1 - st["b6"]
        pb = bank(b6)
        for k in range(KC):
            mm(pb[:, 0:n], ones[:, :], sqap[:, k, :], k == 0, k == KC - 1, [sqbuf, onesb], [bk[b6]])
        r = rs[b6 - 6]
        rb = rsb[b6 - 6]
        act(r[:, 0:n], pb[:, 0:n], AF.Sqrt, reads=[bk[b6], epsb], writes=[rb], bias=epsb_ap[:, 0:1], scale=1.0 / D)
        K.op("dve", lambda e, o=r[:, 0:n]: e.reciprocal(out=o, in_=o), reads=[rb], writes=[rb])
        for k in range(KC):
            dve_stt(hout[:, k, :], xap[:, k, :], tab[:, gcol + k:gcol + k + 1], r[:, 0:n], ALU.mult, ALU.mult,
                    reads=list(xbufs) + [rb, tabb] + ([sqbuf] if sqbuf is hbuf else []), pwrites=[hbuf])

    def evac_copy(out, in_, reads, pwrites):
        st["ev"] ^= 1
        if st["ev"]:
            act(out, in_, AF.Copy, reads=reads, pwrites=pwrites)
        else:
            dve_copy(out, in_, reads=reads, pwrites=pwrites)

    attnTb = K.buf("attnT")
    mark_a = A.lo
    acc = A.f32(2 * 4 * TQ).rearrange("p (a h t) -> p a h t", a=2, h=4)
    accb = K.buf("acc")
    K.op("dve", lambda e: e.memset(acc, 0.0), writes=[accb])
    mark_g = A.lo
    sm_scale = 1.0 / np.sqrt(128.0)
    xTv = xT.rearrange("(k p) t -> p k t", p=128)

    for g in (2, 1, 0):
        A.lo = mark_g
        d = DIL[g]
        H = HALO[g]
        NK = H + TQ
        mk = A.f32(4 * 256).rearrange("p (h c) -> p h c", h=4)
        mkb = K.buf("mk%d" % g, dma=True)
        CH = 192
        xs = [A.f32(16 * CH).rearrange("p (k t) -> p k t", k=16) for _ in range(2)]
        xsb = [K.buf("xs%d_%d" % (g, i), dma=True) for i in range(2)]
        if g == 0:
            h1x = A.bf16(16 * NX1, top=True).rearrange("p (k t) -> p k t", k=16)
            rsf = A.f32(NX1 + 6, top=True)
            h1xb = (K.buf("h1xe"), K.buf("h1xo"))
            rsfb = K.buf("rsf")
            hc = hcb = None
        else:
            hc = [A.bf16(16 * CH).rearrange("p (k t) -> p k t", k=16) for _ in range(2)]
            hcb = [(K.buf("hce%d_%d" % (g, i)), K.buf("hco%d_%d" % (g, i))) for i in range(2)]
        sqs = [A.bf16(4 * CH).rearrange("p (k t) -> p k t", k=4) for _ in range(8)]
        sqsb = [K.buf("sq%d_%d" % (g, i)) for i in range(8)]
        Lk = -(-NK // d)
        Lq = -(-TQ // d)
        kT = A.bf16(4 * d * Lk).rearrange("p (h t) -> p h t", h=4)
        vT = A.bf16(4 * d * Lk).rearrange("p (h t) -> p h t", h=4)
        qT = A.bf16(4 * d * Lq).rearrange("p (h t) -> p h t", h=4)
        kTb, vTb, qTb = K.buf("kT%d" % g), K.buf("vT%d" % g), K.buf("qT%d" % g)
        NS = 4
        Ssb = [A.f32(256) for _ in range(NS)]
        Ssbb = [K.buf("Ssb%d_%d" % (g, i)) for i in range(NS)]
        PT = [A.bf16(256) for _ in range(NS)]
        PTb = [K.buf("PT%d_%d" % (g, i)) for i in range(NS)]
        Vsb = [A.bf16(256) for _ in range(NS)]
        Vsbb = [K.buf("Vsb%d_%d" % (g, i)) for i in range(NS)]

        dma("sp", mk, masksd[:, g * 1024:(g + 1) * 1024].rearrange("p (h c) -> p h c", h=4), mkb, writes=[mkb])
        Wk, Wkb = load_w(w_in, 0, 16, OK_ + g * 512, 512)
        Wv, Wvb = load_w(w_in, 0, 16, OV + g * 512, 512)
        Wq, Wqb = load_w(w_in, 0, 16, OQ + g * 512, 512)

        own_ch = split_mult(J0, TQ - 2, CH - 2, d)
        own_ch[-1] = (own_ch[-1][0], own_ch[-1][1] + 2)
        chunks = split_mult(J0 - H, H, CH, d) + own_ch
        sqn = [0]

        def stage1(ci):
            j0, n = chunks[ci]
            x_, xb_ = xs[ci % 2], xsb[ci % 2]
            dma("sp", x_[:, :, 0:n], xTv[:, :, j0:j0 + n], xb_, writes=[xb_])
            if g == 0:
                jj_ = j0 - (J0 - H)
                h = h1x[:, :, jj_:jj_ + n]
                hb = h1xb
            else:
                h = hc[ci % 2]
                hb = hcb[ci % 2]
            for k in range(KC):
                gk = tab[:, TB_GMIX + k:TB_GMIX + k + 1]
                first = (k < 2) and g != 0
                hbk = hb[k % 2]
                if k % 2 == 0:
                    K.op("dve", lambda e, o=h[:, k, 0:n], i=x_[:, k, 0:n], s_=gk: e.tensor_scalar_mul(out=o, in0=i, scalar1=s_),
                         reads=[xb_, tabb], writes=[hbk] if first else (), pwrites=() if first else [hbk])
                else:
                    act(h[:, k, 0:n], x_[:, k, 0:n], AF.Identity, reads=[xb_, tabb], writes=[hbk] if first else (),
                        pwrites=() if first else [hbk], scale=gk)
            sl = []
            for q4 in range(4):
                si = 4 * (ci % 2) + q4
                act(sqs[si][:, :, 0:n], x_[:, 4 * q4:4 * q4 + 4, 0:n], AF.Square, reads=[xb_], writes=[sqsb[si]])
                sl.append(si)
            return sl

        def stage2(ci, sl):
            j0, n = chunks[ci]
            own = j0 >= J0
            jj = j0 - (J0 - H)
            if g == 0:
                h = h1x[:, :, jj:jj + n]
                hb = h1xb
            else:
                h = hc[ci % 2]
                hb = hcb[ci % 2]
            b6 = 6 + st["b6"]
            st["b6"] = 1 - st["b6"]
            pb = bank(b6)
            for q4 in range(4):
                for kk in range(4):
                    k = 4 * q4 + kk
                    mm(pb[:, 0:n], ones[:, :], sqs[sl[q4]][:, kk, 0:n], k == 0, k == KC - 1, [sqsb[sl[q4]], onesb], [bk[b6]])
            if g == 0:
                r = rsf[:, jj:jj + n]
                rb = rsfb
                act(r[:, 0:n], pb[:, 0:n], AF.Sqrt, reads=[bk[b6], epsb], pwrites=[rb], bias=epsb_ap[:, 0:1], scale=1.0 / D)
                K.op("dve", lambda e, o=r[:, 0:n]: e.reciprocal(out=o, in_=o), reads=[rb], pwrites=[rb])
            else:
                r = rs[b6 - 6]
                rb = rsb[b6 - 6]
                act(r[:, 0:n], pb[:, 0:n], AF.Sqrt, reads=[bk[b6], epsb], writes=[rb], bias=epsb_ap[:, 0:1], scale=1.0 / D)
                K.op("dve", lambda e, o=r[:, 0:n]: e.reciprocal(out=o, in_=o), reads=[rb], writes=[rb])
            targets = [(Wk, Wkb, kT, kTb, jj, Lk), (Wv, Wvb, vT, vTb, jj, Lk)]
            if own:
                targets.append((Wq, Wqb, qT, qTb, j0 - J0, Lq))
            nmain = n - (n % d)
            for (W, Wb, dst, dstb, c0, Ld) in targets:
                for m in range(4):
                    b = next_bank6()
                    pbk = bank(b)
                    for k in range(KC):
                        mm(pbk[:, 0:n], W[:, k, m * 128:(m + 1) * 128], h[:, k, 0:n], k == 0, k == KC - 1,
                           [Wb, hb[k % 2]], [bk[b]])
                    if d == 1:
                        dve_tt(dst[:, m, c0:c0 + n], pbk[:, 0:n], r[:, 0:n], ALU.mult, reads=[bk[b], rb], pwrites=[dstb])
                    else:
                        assert c0 % d == 0
                        ov = dst[:, m, :].rearrange("p (r l) -> p r l", r=d)[:, :, c0 // d:c0 // d + nmain // d]
                        dve_tt(ov, pbk[:, 0:nmain].rearrange("p (l r) -> p r l", r=d),
                               r[:, 0:nmain].rearrange("p (l r) -> p r l", r=d), ALU.mult,
                               reads=[bk[b], rb], pwrites=[dstb])
                        if n > nmain:
                            assert n - nmain == 2 and (c0 + nmain) // d == Ld - 1
                            dve_tt(dst[:, m, Ld - 1:2 * Ld:Ld], pbk[:, nmain:n], r[:, nmain:n], ALU.mult,
                                   reads=[bk[b], rb], pwrites=[dstb])

        sls = {0: stage1(0)}
        for ci in range(len(chunks)):
            if ci + 1 < len(chunks):
                sls[ci + 1] = stage1(ci + 1)
            stage2(ci, sls[ci])
        K.barrier()

        def kvcol(start, n):
            key = (g, start)
            if key not in kvcols:
                kvcols[key] = (len(kvcols), n)
            assert kvcols[key][0] < NKV
            return TB_KV + kvcols[key][0]

        tiles = []
        for hh in range(4):
            for rho in range(d):
                M = -(-(TQ - rho) // d)
                for m0 in range(0, M, 128):
                    tiles.append((hh, rho, m0, min(128, M - m0)))

        def geom(t):
            hh, rho, m0, nq = t
            q0 = rho + d * m0
            asl = slice(q0, q0 + d * (nq - 1) + 1, d)
            qsl = slice(rho * Lq + m0, rho * Lq + m0 + nq)
            sp_ = H + rho + d * (m0 - 128)
            sc_ = H + rho + d * m0
            psl = slice(rho * Lk + m0, rho * Lk + m0 + 128)
            csl = slice(rho * Lk + 128 + m0, rho * Lk + 128 + m0 + nq)
            return hh, nq, qsl, sp_, sc_, psl, csl, asl

        def tstage1(ti):
            hh, nq, qsl, sp_, sc_, psl, csl, asl = geom(tiles[ti])
            s_ = ti % NS
            Sb = bk[2 * s_]
            sv = (ti + NS - 1) % NS
            Vb = bk[2 * sv + 1]
            S = bank(2 * s_)[:, 0:256]
            Vp = bank(2 * sv + 1)[:, 256:512]
            qv = qT[:, hh, qsl]
            mm(S[:, 0:nq], kT[:, hh, psl], qv, True, True, [kTb, qTb], [Sb])
            mm(S[0:nq, 128:128 + nq], kT[:, hh, csl], qv, True, True, [kTb, qTb], [Sb])
            mm(Vp[:, 0:128], vT[:, hh, psl], ident[:, :], True, True, [vTb, identb], [Vb])
            mm(Vp[0:nq, 128:256], vT[:, hh, csl], ident[:, :], True, True, [vTb, identb], [Vb])
            ss, ssb_ = Ssb[s_], Ssbb[s_]
            pt, ptb = PT[s_], PTb[s_]
            vs, vsb_ = Vsb[s_], Vsbb[s_]
            if nq == 128:
                dve_stt(ss[:, 0:256], S[:, 0:256], sm_scale, mk[:, hh, 0:256], ALU.mult, ALU.add,
                        reads=[Sb, mkb], writes=[ssb_])
                act(vs[:, 0:256], Vp[:, 0:256], AF.Copy, reads=[Vb], writes=[vsb_])
            else:
                dve_stt(ss[:, 0:nq], S[:, 0:nq], sm_scale, mk[:, hh, 0:nq], ALU.mult, ALU.add,
                        reads=[Sb, mkb], writes=[ssb_])
                dve_stt(ss[0:nq, 128:128 + nq], S[0:nq, 128:128 + nq], sm_scale, mk[0:nq, hh, 128:128 + nq],
                        ALU.mult, ALU.add, reads=[Sb, mkb], pwrites=[ssb_])
                act(vs[:, 0:128], Vp[:, 0:128], AF.Copy, reads=[Vb], writes=[vsb_])
                act(vs[0:nq, 128:256], Vp[0:nq, 128:256], AF.Copy, reads=[Vb], pwrites=[vsb_])
            cp = kvcol(sp_, 128)
            cc = kvcol(sc_, nq)
            act(pt[:, 0:nq], ss[:, 0:nq], AF.Exp, reads=[ssb_, tabb], writes=[ptb], bias=tab[:, cp:cp + 1], scale=1.0)
            act(pt[0:nq, 128:128 + nq], ss[0:nq, 128:128 + nq], AF.Exp, reads=[ssb_, tabb], pwrites=[ptb],
                bias=tab[0:nq, cc:cc + 1], scale=1.0)

        def tstage2(ti):
            hh, nq, qsl, sp_, sc_, psl, csl, asl = geom(tiles[ti])
            s_ = ti % NS
            OLb = bk[2 * s_ + 1]
            OL = bank(2 * s_ + 1)[:, 0:256]
            pt, ptb = PT[s_], PTb[s_]
            vs, vsb_ = Vsb[s_], Vsbb[s_]
            mm(OL[:, 0:nq], vs[:, 0:128], pt[:, 0:nq], True, False, [vsb_, ptb], [OLb])
            mm(OL[:, 0:nq], vs[0:nq, 128:256], pt[0:nq, 128:128 + nq], False, True, [vsb_, ptb], [OLb])
            mm(OL[:, 128:128 + nq], ones[:, :], pt[:, 0:nq], True, False, [onesb, ptb], [OLb])
            mm(OL[:, 128:128 + nq], ones[0:nq, :], pt[0:nq, 128:128 + nq], False, True, [onesb, ptb], [OLb])
            accv = acc[:, :, hh, asl]
            olv = OL.rearrange("p (a c) -> p a c", a=2)[:, :, 0:nq]
            dve_tt(accv, olv, accv, ALU.add, reads=[OLb, accb], pwrites=[accb])

        DEPTH = NS - 1
        for ti in range(len(tiles) + DEPTH):
            if ti < len(tiles):
                tstage1(ti)
            if ti >= DEPTH:
                tstage2(ti - DEPTH)
        K.barrier()

    A.lo = mark_g
    nrm_ops = []
    attnT = A.bf16(4 * TQ, top=True).rearrange("p (h t) -> p h t", h=4)
    for hh in range(4):
        K.op("dve", lambda e, o=acc[:, 1, hh, :]: e.tensor_scalar_max(out=o, in0=o, scalar1=1e-30), reads=[accb], pwrites=[accb])
        K.op("dve", lambda e, o=acc[:, 1, hh, :]: e.reciprocal(out=o, in_=o), reads=[accb], pwrites=[accb])
        nrm_ops.append(dve_tt(attnT[:, hh, :], acc[:, 0, hh, :], acc[:, 1, hh, :], ALU.mult, reads=[accb], pwrites=[attnTb]))
    A.lo = mark_a
    wslot.append(A.bf16(WSLOT))
    wb3 = K.buf("w3", dma=True)
    wb3.keep = True
    wbuf.append(wb3)
    st["w3deps"] = list(K.last_barrier) + nrm_ops
    st["nslots"] = 4
    mark_a = A.lo

    mixedT = A.bf16(16 * TQ).rearrange("p (k t) -> p k t", k=16)
    mixedb = K.buf("mixedT")
    mark_m = A.lo
    poolz = A.bf16(8 * TQ).rearrange("p (k t) -> p k t", k=8)
    poolzb = K.buf("poolz")
    mark_p = A.lo
    CU = 347
    CQ = 342
    U0 = 128 - UH
    pooled2 = [A.bf16(2 * TQ).rearrange("p (k t) -> p k t", k=2) for _ in range(2)]
    pooled2b = [K.buf("pooled%d" % i) for i in range(2)]
    usb = [A.f32(TU + 7) for _ in range(2)]
    usbb = [K.buf("usb%d" % i) for i in range(2)]
    tmpa = [A.f32(TU + 7) for _ in range(2)]
    tmpab = [K.buf("tmpa%d" % i) for i in range(2)]

    def v3(ap):
        return ap.rearrange("p (c t) -> p c t", c=3)

    for blk in range(2):
        W, Wb = load_w(w_in, 0, 16, blk * 512, 512)
        for mi in range(4):
            m = blk * 4 + mi
            gi = next_grp()
            G = grp(gi, 3, CU)
            for k in range(KC):
                for c in range(3):
                    mm(G[:, c, :], W[:, k, mi * 128:(mi + 1) * 128], h1x[:, k, U0 + c * CU:U0 + (c + 1) * CU], k == 0, k == KC - 1,
                       [Wb, h1xb[k % 2]], grpb[gi])
            u = usb[m % 2]
            ub = usbb[m % 2]
            dve_tt(v3(u[:, 0:TU]), G, v3(rsf[:, U0:U0 + TU]), ALU.mult, reads=grpb[gi] + [rsfb], writes=[ub])
            w = POOLW[m // 2]
            cur, curb = u, ub
            vf = 0
            span = 1
            ti = 0
            while span < w:
                nx, nxb = tmpa[ti % 2], tmpab[ti % 2]
                ti += 1
                dve_tt(nx[:, vf + span:TU], cur[:, vf + span:TU], cur[:, vf:TU - span], ALU.add, reads=[curb], writes=[nxb])
                cur, curb = nx, nxb
                vf += span
                span *= 2
            cs = TB_CORR + (m // 2) * 18
            dve_tt(cur[:, UH:UH + 18], cur[:, UH:UH + 18], tab[:, cs:cs + 18], ALU.mult, reads=[curb, tabb], pwrites=[curb])
            gi4 = m // 2
            pd, pdb = pooled2[gi4 % 2], pooled2b[gi4 % 2]
            dve_stt(pd[:, m % 2, :], cur[:, UH:TU], 1.0 / w, u[:, UH:TU], ALU.mult, ALU.subtract,
                    reads=[curb, ub], writes=[pdb] if m % 2 == 0 else (), pwrites=() if m % 2 == 0 else [pdb])
            if m % 2 == 1:
                Wl, Wlb = load_w(wpl, gi4 * 2, 2, 0, 256)
                for mj in range(2):
                    mo = gi4 * 2 + mj
                    gj = next_grp()
                    Gl = grp(gj, 3, CQ)
                    for k in range(2):
                        for c in range(3):
                            mm(Gl[:, c, :], Wl[:, k, mj * 128:(mj + 1) * 128], pd[:, k, c * CQ:(c + 1) * CQ],
                               k == 0, k == 1, [Wlb, pdb], grpb[gj])
                    act(v3(poolz[:, mo, :]), Gl, AF.Identity, reads=grpb[gj] + [tabb], pwrites=[poolzb],
                        scale=tab[:, TB_PS + mo:TB_PS + mo + 1])
    K.barrier()
    A.lo = mark_p

    sg0 = A.f32(TQ)
    sg1 = A.f32(TQ)
    t0 = A.f32(TQ)
    sg0b, sg1b, t0b = K.buf("sg0"), K.buf("sg1"), K.buf("t0")

    for blk in range(4):
        Wg0, Wg0b = load_w(w_in, 0, 16, OG + blk * 512, 512)
        Wg1, Wg1b = load_w(w_in, 0, 16, OG + D + blk * 512, 512)
        Wpo, Wao, Wpob = load_w2(blk)
        Waob = Wpob
        for mi in range(4):
            m = blk * 4 + mi
            msl = slice(mi * 128, (mi + 1) * 128)
            Msl = slice(m * 128, (m + 1) * 128)
            G = grp(0, 3, CQ)
            for k in range(KC):
                for c in range(3):
                    mm(G[:, c, :], Wg0[:, k, msl], h1x[:, k, 128 + c * CQ:128 + (c + 1) * CQ], k == 0, k == KC - 1,
                       [Wg0b, h1xb[k % 2]], grpb[0])
            dve_tt(v3(sg0[:, :]), G, v3(rsf[:, 128:128 + TQ]), ALU.mult, reads=grpb[0] + [rsfb], writes=[sg0b])
            act(sg0[:, :], sg0[:, :], AF.Sigmoid, reads=[sg0b, tabb], writes=[sg0b], bias=tab[:, TB_BG + m:TB_BG + m + 1], scale=1.0)
            G1 = grp(1, 3, CQ)
            for k in range(KC):
                for c in range(3):
                    mm(G1[:, c, :], Wg1[:, k, msl], h1x[:, k, 128 + c * CQ:128 + (c + 1) * CQ], k == 0, k == KC - 1,
                       [Wg1b, h1xb[k % 2]], grpb[1])
            dve_tt(v3(sg1[:, :]), G1, v3(rsf[:, 128:128 + TQ]), ALU.mult, reads=grpb[1] + [rsfb], writes=[sg1b])
            act(sg1[:, :], sg1[:, :], AF.Sigmoid, reads=[sg1b, tabb], writes=[sg1b],
                bias=tab[:, TB_BG + 16 + m:TB_BG + 16 + m + 1], scale=1.0)
            for k in range(8):
                for c in range(3):
                    mm(G[:, c, :], Wpo[:, k, msl], poolz[:, k, c * CQ:(c + 1) * CQ], k == 0, k == 7, [Wpob, poolzb], grpb[0])
            dve_tt(v3(t0[:, :]), G, v3(sg0[:, :]), ALU.mult, reads=grpb[0] + [sg0b], writes=[t0b])
            for k in range(4):
                for c in range(3):
                    mm(G1[:, c, :], Wao[:, k, msl], attnT[:, k, c * CQ:(c + 1) * CQ], k == 0, k == 3, [Waob, attnTb], grpb[1])
            dve_tt(v3(sg1[:, :]), G1, v3(sg1[:, :]), ALU.mult, reads=grpb[1] + [sg1b], writes=[sg1b])
            dve_tt(mixedT[:, m, :], t0[:, :], sg1[:, :], ALU.add, reads=[t0b, sg1b], pwrites=[mixedb])
    K.barrier()
    A.lo = mark_m
    A.hi = ARENA_WORDS

    x1T = A.f32(16 * TQ, top=True).rearrange("p (k t) -> p k t", k=16)
    x1b = [K.buf("x1_%d" % m) for m in range(16)]
    xr = [A.f32(TQ) for _ in range(2)]
    xrb = [K.buf("xr%d" % i, dma=True) for i in range(2)]
    for blk in range(4):
        W, Wb = load_w(w_out, 0, 16, blk * 512, 512)
        for mi in range(4):
            m = blk * 4 + mi
            gi = next_grp()
            G = grp(gi, 3, CQ)
            dma("sp", xr[m % 2], xT[m * 128:(m + 1) * 128, J0:J0 + TQ], xrb[m % 2], writes=[xrb[m % 2]])
            for k in range(KC):
                for c in range(3):
                    mm(G[:, c, :], W[:, k, mi * 128:(mi + 1) * 128], mixedT[:, k, c * CQ:(c + 1) * CQ], k == 0, k == KC - 1,
                       [Wb, mixedb], grpb[gi])
            dve_tt(v3(x1T[:, m, :]), G, v3(xr[m % 2][:, :]), ALU.add, reads=grpb[gi] + [xrb[m % 2]], writes=[x1b[m]])
    K.barrier()
    A.lo = mark_a

    h2T = A.bf16(16 * TQ).rearrange("p (k t) -> p k t", k=16)
    h2b = (K.buf("h2e"), K.buf("h2o"))
    NQF = 11
    prodT_flat = A.bf16(NQF * T)
    prodT = prodT_flat.rearrange("p (k t) -> p k t", k=NQF)
    prodb = K.buf("prodT")
    rs2 = rsblk[:, 0:TQ]
    rs2b = K.buf("rs2")
    sq2 = [prodT_flat[:, i * 4 * CQ:(i + 1) * 4 * CQ].rearrange("p (k t) -> p k t", k=4) for i in range(2)]
    sq2b = [K.buf("sq2_%d" % i) for i in range(2)]
    last_sq_read = None
    for c in range(3):
        cols = slice(c * CQ, (c + 1) * CQ)
        b6 = 6 + st["b6"]
        st["b6"] = 1 - st["b6"]
        pb = bank(b6)
        for q4 in range(4):
            i = (c * 4 + q4) % 2
            act(sq2[i], x1T[:, 4 * q4:4 * q4 + 4, cols], AF.Square, reads=x1b[4 * q4:4 * q4 + 4], writes=[sq2b[i]])
            for kk in range(4):
                k = 4 * q4 + kk
                last_sq_read = mm(pb[:, 0:CQ], ones[:, :], sq2[i][:, kk, :], k == 0, k == KC - 1, [sq2b[i], onesb], [bk[b6]])
        act(rs2[:, cols], pb[:, 0:CQ], AF.Sqrt, reads=[bk[b6], epsb], pwrites=[rs2b], bias=epsb_ap[:, 0:1], scale=1.0 / D)
        K.op("dve", lambda e, o=rs2[:, cols]: e.reciprocal(out=o, in_=o), reads=[rs2b], pwrites=[rs2b])
        for k in range(KC):
            gk = tab[:, TB_GFFN + k:TB_GFFN + k + 1]
            if k % 2 == 0:
                K.op("dve", lambda e, o=h2T[:, k, cols], i_=x1T[:, k, cols], s_=gk: e.tensor_scalar_mul(out=o, in0=i_, scalar1=s_),
                     reads=[x1b[k], tabb], pwrites=[h2b[0]])
            else:
                act(h2T[:, k, cols], x1T[:, k, cols], AF.Identity, reads=[x1b[k], tabb], pwrites=[h2b[1]], scale=gk)

    ga = A.f32(T).rearrange("p (k t) -> p k t", k=1)
    gab = [K.buf("ga0"), K.buf("ga1")]
    preS = [A.f32(TQ)] * 2
    preSb = [K.buf("preS0")] * 2
    c0t = [A.f32(TQ) for _ in range(2)]
    c0b = [K.buf("c0_%d" % i) for i in range(2)]
    cvn = 0

    def conv_chunk(W, Wb, ci, f):
        nonlocal cvn
        gi = next_grp()
        G = grp(gi, 3, CQ)
        for k in range(KC):
            for c in range(3):
                mm(G[:, c, :], W[:, k, ci * 128:(ci + 1) * 128], h2T[:, k, c * CQ:(c + 1) * CQ], k == 0, k == KC - 1,
                   [Wb, h2b[k % 2]], grpb[gi])
        p = cvn % 2
        cvn += 1
        ps_, psb_ = preS[p], preSb[p]
        cc, ccb = c0t[p], c0b[p]
        dve_tt(v3(ps_[:, :]), G, v3(rs2), ALU.mult, reads=grpb[gi] + [rs2b], writes=[psb_])
        act(cc[:, :], ps_[:, :], AF.Identity, reads=[psb_, tabb], writes=[ccb],
            bias=tab[:, TB_CB + f:TB_CB + f + 1], scale=tab[:, TB_CW + 2 * 88 + f:TB_CW + 2 * 88 + f + 1])
        dve_stt(cc[:, 1:TQ], ps_[:, 0:TQ - 1], tab[:, TB_CW + 88 + f:TB_CW + 88 + f + 1], cc[:, 1:TQ], ALU.mult, ALU.add,
                reads=[psb_, ccb, tabb], writes=[ccb])
        dve_stt(cc[:, 2:TQ], ps_[:, 0:TQ - 2], tab[:, TB_CW + f:TB_CW + f + 1], cc[:, 2:TQ], ALU.mult, ALU.add,
                reads=[psb_, ccb, tabb], writes=[ccb])
        return cc, ccb

    def load_wab(f):
        dst, wb_ = load_flat(w_up[f * 128:(f + 1) * 128, :], 4096)
        return (dst[:, 0:2048].rearrange("p (k n) -> p k n", k=16),
                dst[:, 2048:4096].rearrange("p (k n) -> p k n", k=16), wb_)

    units = []
    for qq in range(4):
        for f in range(qq * NQF, (qq + 1) * NQF):
            units.append(("A", f))
            units.append(("B", f))
        units.append(("D", qq))
    early = set()
    for i in range(len(units) - 1):
        if units[i][0] == "D" and units[i + 1][0] == "A":
            units[i], units[i + 1] = units[i + 1], units[i]
            early.add(units[i][1])
    wabs = {}
    for kind, v in units:
        if kind == "A":
            f = v
            if f in early:
                Wa, Wab = load_w(w_up, 0, 16, f * 128, 128)
            else:
                wabs[f] = load_wab(f)
                Wa, Wbk, Wab = wabs[f]
            cc, ccb = conv_chunk(Wa, Wab, 0, f)
            act(ga[:, 0, :], cc[:, 2:TQ], AF.Gelu, reads=[ccb], writes=[gab[0]])
        elif kind == "B":
            f = v
            if f in early:
                Wbk, Wab = load_w(w_up, 0, 16, DFF + f * 128, 128)
            else:
                Wa, Wbk, Wab = wabs.pop(f)
            f0 = (f // NQF) * NQF
            cc, ccb = conv_chunk(Wbk, Wab, 0, FC + f)
            dve_tt(prodT[:, f - f0, :], cc[:, 2:TQ], ga[:, 0, :], ALU.mult, reads=[ccb, gab[0]], pwrites=[prodb],
                   extra=[last_sq_read] if f == 0 else ())
        else:
            qq = v
            f0 = qq * NQF
            for blk in range(4):
                Wd, Wdb = load_w(w_down, f0, NQF, blk * 512, 512)
                for mi in range(4):
                    m = blk * 4 + mi
                    gi = next_grp()
                    G = grp(gi, 2, 512)
                    for k in range(NQF):
                        for c in range(2):
                            mm(G[:, c, :], Wd[:, k, mi * 128:(mi + 1) * 128], prodT[:, k, c * 512:(c + 1) * 512],
                               k == 0, k == NQF - 1, [Wdb, prodb], grpb[gi])
                    xv = x1T[:, m, 2:TQ].rearrange("p (c t) -> p c t", c=2)
                    dve_tt(xv, G, xv, ALU.add, reads=grpb[gi] + [x1b[m]], writes=[x1b[m]])
    K.barrier()
    A.lo = mark_a

    sqf = A.bf16(16 * 256).rearrange("p (k t) -> p k t", k=16)
    sqfb = K.buf("sqf")
    ost = [A.f32(16 * 256).rearrange("p (k t) -> p k t", k=16) for _ in range(2)]
    ostb = [K.buf("ost%d" % i, dma=True) for i in range(2)]
    for c in range(4):
        xv = x1T[:, :, 2 + c * 256:2 + (c + 1) * 256]
        norm_core(xv, x1b, 256, TB_GF, ost[c % 2], ostb[c % 2], sqf, sqfb)
        dma("sp", outT.rearrange("(k p) t -> p k t", p=128)[:, :, c * 256:(c + 1) * 256], ost[c % 2], ostb[c % 2],
            reads=[ostb[c % 2]])
    K.barrier()
    K.op("sp", None)
    K.emit(nc)
    return nc, kvcols


_CACHE = {}


def _host_tables(inputs, kvcols):
    def colmaj(v, nchunk):
        return np.ascontiguousarray(np.asarray(v, np.float32).reshape(nchunk, 128).T)

    base = np.zeros((128, NTAB), np.float32)
    base[:, TB_GMIX:TB_GMIX + 16] = colmaj(inputs["g_mix"][0], 16)
    base[:, TB_BG:TB_BG + 32] = colmaj(inputs["b_gate"][0], 32)
    base[:, TB_PS:TB_PS + 8] = colmaj(inputs["pool_scale"][0], 8)
    base[:, TB_GFFN:TB_GFFN + 16] = colmaj(inputs["g_ffn"][0], 16)
    cw = np.asarray(inputs["conv_w"][0], np.float32)
    for i in range(3):
        base[:, TB_CW + i * 88:TB_CW + (i + 1) * 88] = colmaj(cw[i], 88)
    base[:, TB_CB:TB_CB + 88] = colmaj(inputs["conv_b"][0], 88)
    base[:, TB_GF:TB_GF + 16] = colmaj(inputs["g_final"], 16)
    tabs = []
    for c in range(NCORES):
        t = base.copy()
        s0 = c * T
        for wi, w in enumerate(POOLW):
            for col in range(18):
                tt = s0 - 2 + col
                t[:, TB_CORR + wi * 18 + col] = (w / min(tt + 1, w)) if tt >= 0 else 1.0
        for (g, start), (idx, n) in kvcols.items():
            d = DIL[g]
            i = np.arange(128)
            tt = s0 - 2 - J0 + (J0 - HALO[g]) + start + d * i
            t[:, TB_KV + idx] = np.where(tt >= 0, 0.0, NEG)
        tabs.append(t)
    masks = np.zeros((128, 12, 256), np.float32)
    kk = np.arange(128)[:, None]
    ii = np.arange(128)[None, :]
    for hh in range(12):
        g = hh // 4
        d = DIL[g]
        slope = np.float32(2.0 ** (-8.0 * (hh + 1) / 12))
        jp = ii + 128 - kk
        masks[:, hh, 0:128] = np.where(kk >= ii, -slope * (jp * d).astype(np.float32), NEG)
        jc = ii - kk
        masks[:, hh, 128:256] = np.where(kk <= ii, -slope * (jc * d).astype(np.float32), NEG)
    return tabs, masks.reshape(128, 12 * 256)


def kernel(**inputs):
    if "prog" not in _CACHE:
        _CACHE["prog"] = build_program()
    nc, kvcols = _CACHE["prog"]
    x = np.asarray(inputs["x"], np.float32)[0]
    S = x.shape[0]
    xTfull = np.zeros((D, J0 + 2 + S), np.float32)
    xTfull[:, J0 + 2:] = x.T
    tabs, masks = _host_tables(inputs, kvcols)
    ident = np.eye(128, dtype=np.float32)
    def tile_blocks(W, nk, ncols):
        W = np.asarray(W, np.float32)
        nb = W.shape[1] // ncols
        return np.ascontiguousarray(W.reshape(nk, 128, nb, ncols).transpose(2, 1, 0, 3)).reshape(nb * 128, nk * ncols)

    w_up_full = np.asarray(inputs["w_up"], np.float32)[0]
    w_down_full = np.asarray(inputs["w_down"], np.float32)[0]
    wplf = np.asarray(inputs["w_pool_lin"], np.float32)[0]
    shared = {
        "w_in": tile_blocks(np.asarray(inputs["w_in"], np.float32)[0], 16, 512),
        "wpl": np.concatenate([tile_blocks(wplf[gi], 2, 256) for gi in range(4)], axis=0),
        "wpa": np.concatenate([tile_blocks(np.asarray(inputs["w_pool_out"], np.float32)[0], 8, 512),
                               tile_blocks(np.asarray(inputs["w_attn_out"], np.float32)[0], 4, 512)], axis=1),
        "w_out": tile_blocks(np.asarray(inputs["w_out"], np.float32)[0], 16, 512),
        "w_up": np.concatenate([tile_blocks(w_up_full[:, :DFF], 16, 128), tile_blocks(w_up_full[:, DFF:], 16, 128)], axis=1),
        "w_down": np.concatenate([tile_blocks(w_down_full[q * 1408:(q + 1) * 1408], 11, 512) for q in range(4)], axis=0),
        "masks": masks,
        "ident": ident,
    }
    in_maps = []
    for c in range(NCORES):
        m = dict(shared)
        m["xT"] = np.ascontiguousarray(xTfull[:, c * T:c * T + NT])
        m["tab"] = tabs[c]
        in_maps.append(m)
    res = run_bass_kernel_spmd(nc, in_maps, core_ids=list(range(NCORES)))
    out = np.empty((1, S, D), np.float32)
    for c in range(NCORES):
        out[0, c * T:(c + 1) * T, :] = np.asarray(res.results[c]["outT"]).T
    return out
```

```python
import numpy as np
import concourse.bass as bass
import concourse.mybir as mybir
from concourse.bass_utils import run_bass_kernel_spmd

F32 = mybir.dt.float32
BF16 = mybir.dt.bfloat16
AF = mybir.ActivationFunctionType
ALU = mybir.AluOpType

NCORES = 8
D = 2048
KC = 16
T = 1024
TQ = 1026
J0 = 2048
NT = J0 + TQ
UH = 15
TU = TQ + UH
OQ, OK_, OV, OG = 1024, 2560, 4096, 5632
INW = 9728
DFF = 5632
FC = 44
DIL = [1, 4, 16]
HALO = [128, 512, 2048]
POOLW = [2, 4, 8, 16]
NEG = -30000.0
EPS = 1e-6
NKV = 96
NX1 = 128 + TQ
TB_GMIX, TB_BG, TB_PS, TB_GFFN, TB_CW, TB_CB, TB_GF, TB_CORR, TB_KV = 0, 16, 48, 56, 72, 336, 424, 440, 512
NTAB = TB_KV + NKV
ARENA_WORDS = 52600
ENGS = ("pe", "act", "dve", "pool", "sp")


class Buf:
    __slots__ = ("name", "lws", "rd", "drd", "sem", "dcount", "dma", "keep")

    def __init__(self, name, dma=False):
        self.name = name
        self.keep = False
        self.lws = []
        self.rd = {}
        self.drd = []
        self.sem = None
        self.dcount = 0
        self.dma = dma


class Op:
    __slots__ = ("eng", "fn", "deps", "is_dma", "dbuf", "event", "sig")

    def __init__(self, eng, fn, is_dma, dbuf):
        self.eng = eng
        self.fn = fn
        self.deps = set()
        self.is_dma = is_dma
        self.dbuf = dbuf
        self.event = None
        self.sig = False


class Kern:
    def __init__(self):
        self.streams = {e: [] for e in ENGS}
        self.bufs = []
        self.pending = {e: [] for e in ENGS}
        self.dma_since = []
        self.last_barrier = []

    def buf(self, name, dma=False):
        b = Buf(name, dma)
        self.bufs.append(b)
        return b

    def op(self, eng, fn, reads=(), writes=(), pwrites=(), dma_buf=None, extra=()):
        o = Op(eng, fn, dma_buf is not None, dma_buf)
        deps = o.deps
        deps.update(extra)
        if self.pending[eng]:
            deps.update(self.pending[eng])
            self.pending[eng] = []
        for b in reads:
            deps.update(b.lws)
        for b in writes:
            deps.update(b.lws)
            deps.update(b.rd.values())
            deps.update(b.drd)
        for b in pwrites:
            deps.update(b.rd.values())
            deps.update(b.drd)
        if not o.is_dma and eng == "pe":
            o.deps = deps = {d for d in deps if d.is_dma or d.eng != "pe"}
        deps.discard(o)
        for b in reads:
            if o.is_dma:
                b.drd.append(o)
            else:
                b.rd[eng] = o
        for b in writes:
            b.lws = [o]
            b.rd = {}
            b.drd = []
        for b in pwrites:
            b.lws.append(o)
        self.streams[eng].append(o)
        if o.is_dma:
            self.dma_since.append(o)
        return o

    def barrier(self):
        last = [s[-1] for e, s in self.streams.items() if s and e != "pool"]
        allops = last + [o for o in self.dma_since if not o.dbuf.keep]
        for e in ENGS:
            if e != "pool":
                self.pending[e] = list(allops)
        self.last_barrier = list(allops)
        self.dma_since = [o for o in self.dma_since if o.dbuf.keep]
        for b in self.bufs:
            if b.keep:
                continue
            b.lws = []
            b.rd = {}
            b.drd = []

    def emit(self, nc):
        for ops in self.streams.values():
            for o in ops:
                for d in o.deps:
                    d.sig = True
        engsem = {e: nc.alloc_semaphore("s_" + e) for e in ENGS}
        for b in self.bufs:
            if b.dma:
                b.sem = nc.alloc_semaphore("d_" + b.name)
        for e in ENGS:
            tick = 0
            for o in self.streams[e]:
                if o.is_dma:
                    b = o.dbuf
                    b.dcount += 1
                    o.event = (b.sem, 16 * b.dcount)
                elif o.sig:
                    tick += 1
                    o.event = (engsem[e], tick)
        streams = self.streams

        def run(e, h):
            seen = {}
            for o in streams[e]:
                need = {}
                for d in o.deps:
                    s, v = d.event
                    if need.get(s, 0) < v:
                        need[s] = v
                for s, v in need.items():
                    if seen.get(s, 0) < v:
                        h.wait_ge(s, v)
                        seen[s] = v
                if o.fn is not None:
                    ins = o.fn(h)
                    if o.is_dma:
                        ins.then_inc(o.event[0], 16)
                    elif o.sig:
                        ins.then_inc(o.event[0], 1)

        with nc.Block() as block:
            @block.tensor
            def _(h):
                run("pe", h)

            @block.scalar
            def _(h):
                run("act", h)

            @block.vector
            def _(h):
                run("dve", h)

            @block.gpsimd
            def _(h):
                run("pool", h)

            @block.sync
            def _(h):
                run("sp", h)


class Arena:
    def __init__(self, ap, nwords):
        self.ap = ap
        self.lo = 0
        self.hi = nwords

    def _take(self, nbytes, top):
        nw = (nbytes + 3) // 4
        nw = (nw + 7) // 8 * 8
        if top:
            self.hi -= nw
            off = self.hi
        else:
            off = self.lo
            self.lo += nw
        assert self.lo <= self.hi, ("SBUF arena overflow", self.lo, self.hi)
        return off, nw

    def f32(self, cols, top=False):
        off, nw = self._take(cols * 4, top)
        return self.ap[:, off:off + cols]

    def bf16(self, cols, top=False):
        assert cols % 2 == 0
        off, nw = self._take(cols * 2, top)
        return self.ap[:, off:off + cols // 2].bitcast(BF16)


def split_mult(start, total, maxc, mult):
    assert total % mult == 0
    units = total // mult
    n = -(-total // (maxc // mult * mult))
    base, rem = divmod(units, n)
    out = []
    s = start
    for i in range(n):
        c = (base + (1 if i < rem else 0)) * mult
        out.append((s, c))
        s += c
    return out


def split_chunks(start, total, maxc):
    n = -(-total // maxc)
    base, rem = divmod(total, n)
    out = []
    s = start
    for i in range(n):
        c = base + (1 if i < rem else 0)
        out.append((s, c))
        s += c
    return out


def build_program():
    nc = bass.Bass("TRN2", target_bir_lowering=False)
    dr = {}

    def din(name, shape):
        dr[name] = nc.dram_tensor(name, list(shape), F32, kind="ExternalInput").ap()
        return dr[name]

    xT = din("xT", (D, NT))
    w_in = din("w_in", (19 * 128, 16 * 512))
    wpl = din("wpl", (4 * 128, 2 * 256))
    wpa = din("wpa", (4 * 128, 12 * 512))
    w_out = din("w_out", (4 * 128, 16 * 512))
    w_up = din("w_up", (FC * 128, 2 * 16 * 128))
    w_down = din("w_down", (16 * 128, 11 * 512))
    tabd = din("tab", (128, NTAB))
    masksd = din("masks", (128, 12 * 256))
    identd = din("ident", (128, 128))
    outT = nc.dram_tensor("outT", [D, T], F32, kind="ExternalOutput").ap()

    K = Kern()
    kvcols = {}

    arena_t = nc.sbuf_tensor("arena", [128, ARENA_WORDS], F32).__enter__()
    psA = nc.psum_tensor("psA", [128, 3, 512], F32).__enter__()
    psB = nc.psum_tensor("psB", [128, 3, 512], F32).__enter__()
    psC = nc.psum_tensor("psC", [128, 512], F32).__enter__()
    psD = nc.psum_tensor("psD", [128, 512], F32).__enter__()
    A = Arena(arena_t[:, :], ARENA_WORDS)

    def bank(i):
        if i < 3:
            return psA[:, i, :]
        if i < 6:
            return psB[:, i - 3, :]
        return psC[:, :] if i == 6 else psD[:, :]

    def grp(gi, nb, w):
        return (psA if gi == 0 else psB)[:, 0:nb, 0:w]

    bk = [K.buf("bank%d" % i) for i in range(8)]
    grpb = [[bk[0], bk[1], bk[2]], [bk[3], bk[4], bk[5]]]
    st = {"g": 0, "sb": 0, "w": 0, "b6": 0, "ev": 0}

    def mm(out, lhsT, rhs, start, stop, reads, writes):
        return K.op("pe", lambda e, o=out, l=lhsT, r=rhs, s=start, t=stop: e.matmul(o, lhsT=l, rhs=r, start=s, stop=t),
             reads=reads, writes=writes)

    def act(out, in_, func, reads, writes=(), pwrites=(), bias=None, scale=None):
        kw = {}
        if bias is not None:
            kw["bias"] = bias
        if scale is not None:
            kw["scale"] = scale
        K.op("act", lambda e, o=out, i=in_, f=func, kw=kw: e.activation(out=o, in_=i, func=f, **kw),
             reads=reads, writes=writes, pwrites=pwrites)

    def dve_tt(out, in0, in1, op, reads, writes=(), pwrites=(), extra=()):
        return K.op("dve", lambda e, o=out, a=in0, b=in1, p=op: e.tensor_tensor(out=o, in0=a, in1=b, op=p),
             reads=reads, writes=writes, pwrites=pwrites, extra=extra)

    def dve_stt(out, in0, scalar, in1, op0, op1, reads, writes=(), pwrites=()):
        K.op("dve", lambda e, o=out, a=in0, s=scalar, b=in1, p0=op0, p1=op1:
             e.scalar_tensor_tensor(out=o, in0=a, scalar=s, in1=b, op0=p0, op1=p1),
             reads=reads, writes=writes, pwrites=pwrites)

    def dve_copy(out, in_, reads, writes=(), pwrites=()):
        K.op("dve", lambda e, o=out, i=in_: e.tensor_copy(out=o, in_=i), reads=reads, writes=writes, pwrites=pwrites)

    def dma(eng, out, in_, buf, reads=(), writes=()):
        K.op(eng, lambda e, o=out, i=in_: e.dma_start(out=o, in_=i), reads=reads, writes=writes, dma_buf=buf)

    tab = A.f32(NTAB)
    tabb = K.buf("tab", dma=True)
    ones = A.bf16(128)
    onesb = K.buf("ones")
    ident = A.bf16(128)
    identb = K.buf("ident", dma=True)
    epsb_ap = A.f32(8)
    epsb = K.buf("eps")
    rsblk = A.f32(1040)
    rs = [rsblk[:, 0:520], rsblk[:, 520:1040]]
    rsb = [K.buf("rs0"), K.buf("rs1")]
    WSLOT = 8192
    wslot = [A.bf16(WSLOT) for _ in range(3)]
    wbuf = [K.buf("w%d" % i, dma=True) for i in range(3)]
    for wb_ in wbuf:
        wb_.keep = True
    st["nslots"] = 3

    dma("sp", tab, tabd, tabb, writes=[tabb])
    dma("pool", ident, identd, identb, writes=[identb])
    K.op("dve", lambda e: e.memset(ones, 1.0), writes=[onesb])
    K.op("dve", lambda e: e.memset(epsb_ap, EPS), writes=[epsb])

    def slot_extra(i):
        if i == 3 and not st.get("w3used"):
            st["w3used"] = True
            return list(st["w3deps"])
        return ()

    def load_flat(src2d, n):
        i = st["w"]
        st["w"] = (i + 1) % st["nslots"]
        assert n <= WSLOT
        dst = wslot[i][:, 0:n]
        K.op("pool", lambda e, o=dst, s_=src2d: e.dma_start(out=o, in_=s_), writes=[wbuf[i]], dma_buf=wbuf[i],
             extra=slot_extra(i))
        return dst, wbuf[i]

    def load_w(W, k0, nk, c0, ncols):
        if W is w_in or W is w_out:
            assert k0 == 0 and nk == 16 and ncols == 512 and c0 % 512 == 0
            b_ = c0 // 512
            dst, wb_ = load_flat(W[b_ * 128:(b_ + 1) * 128, :], 8192)
        elif W is wpl:
            assert nk == 2 and ncols == 256 and c0 == 0
            gi_ = k0 // 2
            dst, wb_ = load_flat(W[gi_ * 128:(gi_ + 1) * 128, :], 512)
        elif W is w_down:
            assert nk == 11 and ncols == 512
            rb_ = (k0 // 11) * 4 + c0 // 512
            dst, wb_ = load_flat(W[rb_ * 128:(rb_ + 1) * 128, :], 11 * 512)
        elif W is w_up:
            assert k0 == 0 and nk == 16 and ncols == 128
            half_ = 0 if c0 < DFF else 1
            f_ = (c0 - half_ * DFF) // 128
            dst, wb_ = load_flat(W[f_ * 128:(f_ + 1) * 128, half_ * 2048:(half_ + 1) * 2048], 2048)
        else:
            raise AssertionError("unknown weight")
        return dst.rearrange("p (k n) -> p k n", k=nk), wb_

    def load_w2(blk):
        dst, wb_ = load_flat(wpa[blk * 128:(blk + 1) * 128, :], 12 * 512)
        return (dst[:, 0:4096].rearrange("p (k n) -> p k n", k=8),
                dst[:, 4096:6144].rearrange("p (k n) -> p k n", k=4), wb_)

    def next_grp():
        g = st["g"]
        st["g"] = 1 - g
        return g

    def next_bank6():
        b = st["sb"]
        st["sb"] = (b + 1) % 6
        return b

    def norm_core(xap, xbufs, n, gcol, hout, hbuf, sqap, sqbuf):
        act(sqap, xap, AF.Square, reads=xbufs, pwrites=[sqbuf])
        b6 = 6 + st["b6"]
        st["b6"] = 1 - st["b6"]
        pb = bank(b6)
        for k in range(KC):
            mm(pb[:, 0:n], ones[:, :], sqap[:, k, :], k == 0, k == KC - 1, [sqbuf, onesb], [bk[b6]])
        r = rs[b6 - 6]
        rb = rsb[b6 - 6]
        act(r[:, 0:n], pb[:, 0:n], AF.Sqrt, reads=[bk[b6], epsb], writes=[rb], bias=epsb_ap[:, 0:1], scale=1.0 / D)
        K.op("dve", lambda e, o=r[:, 0:n]: e.reciprocal(out=o, in_=o), reads=[rb], writes=[rb])
        for k in range(KC):
            dve_stt(hout[:, k, :], xap[:, k, :], tab[:, gcol + k:gcol + k + 1], r[:, 0:n], ALU.mult, ALU.mult,
                    reads=list(xbufs) + [rb, tabb] + ([sqbuf] if sqbuf is hbuf else []), pwrites=[hbuf])

    def evac_copy(out, in_, reads, pwrites):
        st["ev"] ^= 1
        if st["ev"]:
            act(out, in_, AF.Copy, reads=reads, pwrites=pwrites)
        else:
            dve_copy(out, in_, reads=reads, pwrites=pwrites)

    attnTb = K.buf("attnT")
    mark_a = A.lo
    acc = A.f32(2 * 4 * TQ).rearrange("p (a h t) -> p a h t", a=2, h=4)
    accb = K.buf("acc")
    K.op("dve", lambda e: e.memset(acc, 0.0), writes=[accb])
    mark_g = A.lo
    sm_scale = 1.0 / np.sqrt(128.0)
    xTv = xT.rearrange("(k p) t -> p k t", p=128)

    for g in (2, 1, 0):
        A.lo = mark_g
        d = DIL[g]
        H = HALO[g]
        NK = H + TQ
        mk = A.f32(4 * 256).rearrange("p (h c) -> p h c", h=4)
        mkb = K.buf("mk%d" % g, dma=True)
        CH = 192
        xs = [A.f32(16 * CH).rearrange("p (k t) -> p k t", k=16) for _ in range(2)]
        xsb = [K.buf("xs%d_%d" % (g, i), dma=True) for i in range(2)]
        if g == 0:
            h1x = A.bf16(16 * NX1, top=True).rearrange("p (k t) -> p k t", k=16)
            rsf = A.f32(NX1 + 6, top=True)
            h1xb = (K.buf("h1xe"), K.buf("h1xo"))
            rsfb = K.buf("rsf")
            hc = hcb = None
        else:
            hc = [A.bf16(16 * CH).rearrange("p (k t) -> p k t", k=16) for _ in range(2)]
            hcb = [(K.buf("hce%d_%d" % (g, i)), K.buf("hco%d_%d" % (g, i))) for i in range(2)]
        sqs = [A.bf16(4 * CH).rearrange("p (k t) -> p k t", k=4) for _ in range(8)]
        sqsb = [K.buf("sq%d_%d" % (g, i)) for i in range(8)]
        Lk = -(-NK // d)
        Lq = -(-TQ // d)
        kT = A.bf16(4 * d * Lk).rearrange("p (h t) -> p h t", h=4)
        vT = A.bf16(4 * d * Lk).rearrange("p (h t) -> p h t", h=4)
        qT = A.bf16(4 * d * Lq).rearrange("p (h t) -> p h t", h=4)
        kTb, vTb, qTb = K.buf("kT%d" % g), K.buf("vT%d" % g), K.buf("qT%d" % g)
        NS = 4
        Ssb = [A.f32(256) for _ in range(NS)]
        Ssbb = [K.buf("Ssb%d_%d" % (g, i)) for i in range(NS)]
        PT = [A.bf16(256) for _ in range(NS)]
        PTb = [K.buf("PT%d_%d" % (g, i)) for i in range(NS)]
        Vsb = [A.bf16(256) for _ in range(NS)]
        Vsbb = [K.buf("Vsb%d_%d" % (g, i)) for i in range(NS)]

        dma("sp", mk, masksd[:, g * 1024:(g + 1) * 1024].rearrange("p (h c) -> p h c", h=4), mkb, writes=[mkb])
        Wk, Wkb = load_w(w_in, 0, 16, OK_ + g * 512, 512)
        Wv, Wvb = load_w(w_in, 0, 16, OV + g * 512, 512)
        Wq, Wqb = load_w(w_in, 0, 16, OQ + g * 512, 512)

        own_ch = split_mult(J0, TQ - 2, CH - 2, d)
        own_ch[-1] = (own_ch[-1][0], own_ch[-1][1] + 2)
        chunks = split_mult(J0 - H, H, CH, d) + own_ch
        sqn = [0]

        def stage1(ci):
            j0, n = chunks[ci]
            x_, xb_ = xs[ci % 2], xsb[ci % 2]
            dma("sp", x_[:, :, 0:n], xTv[:, :, j0:j0 + n], xb_, writes=[xb_])
            if g == 0:
                jj_ = j0 - (J0 - H)
                h = h1x[:, :, jj_:jj_ + n]
                hb = h1xb
            else:
                h = hc[ci % 2]
                hb = hcb[ci % 2]
            for k in range(KC):
                gk = tab[:, TB_GMIX + k:TB_GMIX + k + 1]
                first = (k < 2) and g != 0
                hbk = hb[k % 2]
                if k % 2 == 0:
                    K.op("dve", lambda e, o=h[:, k, 0:n], i=x_[:, k, 0:n], s_=gk: e.tensor_scalar_mul(out=o, in0=i, scalar1=s_),
                         reads=[xb_, tabb], writes=[hbk] if first else (), pwrites=() if first else [hbk])
                else:
                    act(h[:, k, 0:n], x_[:, k, 0:n], AF.Identity, reads=[xb_, tabb], writes=[hbk] if first else (),
                        pwrites=() if first else [hbk], scale=gk)
            sl = []
            for q4 in range(4):
                si = 4 * (ci % 2) + q4
                act(sqs[si][:, :, 0:n], x_[:, 4 * q4:4 * q4 + 4, 0:n], AF.Square, reads=[xb_], writes=[sqsb[si]])
                sl.append(si)
            return sl

        def stage2(ci, sl):
            j0, n = chunks[ci]
            own = j0 >= J0
            jj = j0 - (J0 - H)
            if g == 0:
                h = h1x[:, :, jj:jj + n]
                hb = h1xb
            else:
                h = hc[ci % 2]
                hb = hcb[ci % 2]
            b6 = 6 + st["b6"]
            st["b6"] = 1 - st["b6"]
            pb = bank(b6)
            for q4 in range(4):
                for kk in range(4):
                    k = 4 * q4 + kk
                    mm(pb[:, 0:n], ones[:, :], sqs[sl[q4]][:, kk, 0:n], k == 0, k == KC - 1, [sqsb[sl[q4]], onesb], [bk[b6]])
            if g == 0:
                r = rsf[:, jj:jj + n]
                rb = rsfb
                act(r[:, 0:n], pb[:, 0:n], AF.Sqrt, reads=[bk[b6], epsb], pwrites=[rb], bias=epsb_ap[:, 0:1], scale=1.0 / D)
                K.op("dve", lambda e, o=r[:, 0:n]: e.reciprocal(out=o, in_=o), reads=[rb], pwrites=[rb])
            else:
                r = rs[b6 - 6]
                rb = rsb[b6 - 6]
                act(r[:, 0:n], pb[:, 0:n], AF.Sqrt, reads=[bk[b6], epsb], writes=[rb], bias=epsb_ap[:, 0:1], scale=1.0 / D)
                K.op("dve", lambda e, o=r[:, 0:n]: e.reciprocal(out=o, in_=o), reads=[rb], writes=[rb])
            targets = [(Wk, Wkb, kT, kTb, jj, Lk), (Wv, Wvb, vT, vTb, jj, Lk)]
            if own:
                targets.append((Wq, Wqb, qT, qTb, j0 - J0, Lq))
            nmain = n - (n % d)
            for (W, Wb, dst, dstb, c0, Ld) in targets:
                for m in range(4):
                    b = next_bank6()
                    pbk = bank(b)
                    for k in range(KC):
                        mm(pbk[:, 0:n], W[:, k, m * 128:(m + 1) * 128], h[:, k, 0:n], k == 0, k == KC - 1,
                           [Wb, hb[k % 2]], [bk[b]])
                    if d == 1:
                        dve_tt(dst[:, m, c0:c0 + n], pbk[:, 0:n], r[:, 0:n], ALU.mult, reads=[bk[b], rb], pwrites=[dstb])
                    else:
                        assert c0 % d == 0
                        ov = dst[:, m, :].rearrange("p (r l) -> p r l", r=d)[:, :, c0 // d:c0 // d + nmain // d]
                        dve_tt(ov, pbk[:, 0:nmain].rearrange("p (l r) -> p r l", r=d),
                               r[:, 0:nmain].rearrange("p (l r) -> p r l", r=d), ALU.mult,
                               reads=[bk[b], rb], pwrites=[dstb])
                        if n > nmain:
                            assert n - nmain == 2 and (c0 + nmain) // d == Ld - 1
                            dve_tt(dst[:, m, Ld - 1:2 * Ld:Ld], pbk[:, nmain:n], r[:, nmain:n], ALU.mult,
                                   reads=[bk[b], rb], pwrites=[dstb])

        sls = {0: stage1(0)}
        for ci in range(len(chunks)):
            if ci + 1 < len(chunks):
                sls[ci + 1] = stage1(ci + 1)
            stage2(ci, sls[ci])
        K.barrier()

        def kvcol(start, n):
            key = (g, start)
            if key not in kvcols:
                kvcols[key] = (len(kvcols), n)
            assert kvcols[key][0] < NKV
            return TB_KV + kvcols[key][0]

        tiles = []
        for hh in range(4):
            for rho in range(d):
                M = -(-(TQ - rho) // d)
                for m0 in range(0, M, 128):
                    tiles.append((hh, rho, m0, min(128, M - m0)))

        def geom(t):
            hh, rho, m0, nq = t
            q0 = rho + d * m0
            asl = slice(q0, q0 + d * (nq - 1) + 1, d)
            qsl = slice(rho * Lq + m0, rho * Lq + m0 + nq)
            sp_ = H + rho + d * (m0 - 128)
            sc_ = H + rho + d * m0
            psl = slice(rho * Lk + m0, rho * Lk + m0 + 128)
            csl = slice(rho * Lk + 128 + m0, rho * Lk + 128 + m0 + nq)
            return hh, nq, qsl, sp_, sc_, psl, csl, asl

        def tstage1(ti):
            hh, nq, qsl, sp_, sc_, psl, csl, asl = geom(tiles[ti])
            s_ = ti % NS
            Sb = bk[2 * s_]
            sv = (ti + NS - 1) % NS
            Vb = bk[2 * sv + 1]
            S = bank(2 * s_)[:, 0:256]
            Vp = bank(2 * sv + 1)[:, 256:512]
            qv = qT[:, hh, qsl]
            mm(S[:, 0:nq], kT[:, hh, psl], qv, True, True, [kTb, qTb], [Sb])
            mm(S[0:nq, 128:128 + nq], kT[:, hh, csl], qv, True, True, [kTb, qTb], [Sb])
            mm(Vp[:, 0:128], vT[:, hh, psl], ident[:, :], True, True, [vTb, identb], [Vb])
            mm(Vp[0:nq, 128:256], vT[:, hh, csl], ident[:, :], True, True, [vTb, identb], [Vb])
            ss, ssb_ = Ssb[s_], Ssbb[s_]
            pt, ptb = PT[s_], PTb[s_]
            vs, vsb_ = Vsb[s_], Vsbb[s_]
            if nq == 128:
                dve_stt(ss[:, 0:256], S[:, 0:256], sm_scale, mk[:, hh, 0:256], ALU.mult, ALU.add,
                        reads=[Sb, mkb], writes=[ssb_])
                act(vs[:, 0:256], Vp[:, 0:256], AF.Copy, reads=[Vb], writes=[vsb_])
            else:
                dve_stt(ss[:, 0:nq], S[:, 0:nq], sm_scale, mk[:, hh, 0:nq], ALU.mult, ALU.add,
                        reads=[Sb, mkb], writes=[ssb_])
                dve_stt(ss[0:nq, 128:128 + nq], S[0:nq, 128:128 + nq], sm_scale, mk[0:nq, hh, 128:128 + nq],
                        ALU.mult, ALU.add, reads=[Sb, mkb], pwrites=[ssb_])
                act(vs[:, 0:128], Vp[:, 0:128], AF.Copy, reads=[Vb], writes=[vsb_])
                act(vs[0:nq, 128:256], Vp[0:nq, 128:256], AF.Copy, reads=[Vb], pwrites=[vsb_])
            cp = kvcol(sp_, 128)
            cc = kvcol(sc_, nq)
            act(pt[:, 0:nq], ss[:, 0:nq], AF.Exp, reads=[ssb_, tabb], writes=[ptb], bias=tab[:, cp:cp + 1], scale=1.0)
            act(pt[0:nq, 128:128 + nq], ss[0:nq, 128:128 + nq], AF.Exp, reads=[ssb_, tabb], pwrites=[ptb],
                bias=tab[0:nq, cc:cc + 1], scale=1.0)

        def tstage2(ti):
            hh, nq, qsl, sp_, sc_, psl, csl, asl = geom(tiles[ti])
            s_ = ti % NS
            OLb = bk[2 * s_ + 1]
            OL = bank(2 * s_ + 1)[:, 0:256]
            pt, ptb = PT[s_], PTb[s_]
            vs, vsb_ = Vsb[s_], Vsbb[s_]
            mm(OL[:, 0:nq], vs[:, 0:128], pt[:, 0:nq], True, False, [vsb_, ptb], [OLb])
            mm(OL[:, 0:nq], vs[0:nq, 128:256], pt[0:nq, 128:128 + nq], False, True, [vsb_, ptb], [OLb])
            mm(OL[:, 128:128 + nq], ones[:, :], pt[:, 0:nq], True, False, [onesb, ptb], [OLb])
            mm(OL[:, 128:128 + nq], ones[0:nq, :], pt[0:nq, 128:128 + nq], False, True, [onesb, ptb], [OLb])
            accv = acc[:, :, hh, asl]
            olv = OL.rearrange("p (a c) -> p a c", a=2)[:, :, 0:nq]
            dve_tt(accv, olv, accv, ALU.add, reads=[OLb, accb], pwrites=[accb])

        DEPTH = NS - 1
        for ti in range(len(tiles) + DEPTH):
            if ti < len(tiles):
                tstage1(ti)
            if ti >= DEPTH:
                tstage2(ti - DEPTH)
        K.barrier()

    A.lo = mark_g
    nrm_ops = []
    attnT = A.bf16(4 * TQ, top=True).rearrange("p (h t) -> p h t", h=4)
    for hh in range(4):
        K.op("dve", lambda e, o=acc[:, 1, hh, :]: e.tensor_scalar_max(out=o, in0=o, scalar1=1e-30), reads=[accb], pwrites=[accb])
        act(acc[:, 1, hh, :], acc[:, 1, hh, :], AF.Ln, reads=[accb], pwrites=[accb])
        act(acc[:, 1, hh, :], acc[:, 1, hh, :], AF.Exp, reads=[accb], pwrites=[accb], scale=-1.0)
        nrm_ops.append(dve_tt(attnT[:, hh, :], acc[:, 0, hh, :], acc[:, 1, hh, :], ALU.mult, reads=[accb], pwrites=[attnTb]))
    A.lo = mark_a
    wslot.append(A.bf16(WSLOT))
    wb3 = K.buf("w3", dma=True)
    wb3.keep = True
    wbuf.append(wb3)
    st["w3deps"] = list(K.last_barrier) + nrm_ops
    st["nslots"] = 4
    mark_a = A.lo

    mixedT = A.bf16(16 * TQ).rearrange("p (k t) -> p k t", k=16)
    mixedb = K.buf("mixedT")
    mark_m = A.lo
    poolz = A.bf16(8 * TQ).rearrange("p (k t) -> p k t", k=8)
    poolzb = K.buf("poolz")
    mark_p = A.lo
    CU = 347
    CQ = 342
    U0 = 128 - UH
    pooled2 = [A.bf16(2 * TQ).rearrange("p (k t) -> p k t", k=2) for _ in range(2)]
    pooled2b = [K.buf("pooled%d" % i) for i in range(2)]
    usb = [A.f32(TU + 7) for _ in range(2)]
    usbb = [K.buf("usb%d" % i) for i in range(2)]
    tmpa = [A.f32(TU + 7) for _ in range(2)]
    tmpab = [K.buf("tmpa%d" % i) for i in range(2)]

    def v3(ap):
        return ap.rearrange("p (c t) -> p c t", c=3)

    for blk in range(2):
        W, Wb = load_w(w_in, 0, 16, blk * 512, 512)
        for mi in range(4):
            m = blk * 4 + mi
            gi = next_grp()
            G = grp(gi, 3, CU)
            for k in range(KC):
                for c in range(3):
                    mm(G[:, c, :], W[:, k, mi * 128:(mi + 1) * 128], h1x[:, k, U0 + c * CU:U0 + (c + 1) * CU], k == 0, k == KC - 1,
                       [Wb, h1xb[k % 2]], grpb[gi])
            u = usb[m % 2]
            ub = usbb[m % 2]
            dve_tt(v3(u[:, 0:TU]), G, v3(rsf[:, U0:U0 + TU]), ALU.mult, reads=grpb[gi] + [rsfb], writes=[ub])
            w = POOLW[m // 2]
            cur, curb = u, ub
            vf = 0
            span = 1
            ti = 0
            while span < w:
                nx, nxb = tmpa[ti % 2], tmpab[ti % 2]
                ti += 1
                dve_tt(nx[:, vf + span:TU], cur[:, vf + span:TU], cur[:, vf:TU - span], ALU.add, reads=[curb], writes=[nxb])
                cur, curb = nx, nxb
                vf += span
                span *= 2
            cs = TB_CORR + (m // 2) * 18
            dve_tt(cur[:, UH:UH + 18], cur[:, UH:UH + 18], tab[:, cs:cs + 18], ALU.mult, reads=[curb, tabb], pwrites=[curb])
            gi4 = m // 2
            pd, pdb = pooled2[gi4 % 2], pooled2b[gi4 % 2]
            dve_stt(pd[:, m % 2, :], cur[:, UH:TU], 1.0 / w, u[:, UH:TU], ALU.mult, ALU.subtract,
                    reads=[curb, ub], writes=[pdb] if m % 2 == 0 else (), pwrites=() if m % 2 == 0 else [pdb])
            if m % 2 == 1:
                Wl, Wlb = load_w(wpl, gi4 * 2, 2, 0, 256)
                for mj in range(2):
                    mo = gi4 * 2 + mj
                    gj = next_grp()
                    Gl = grp(gj, 3, CQ)
                    for k in range(2):
                        for c in range(3):
                            mm(Gl[:, c, :], Wl[:, k, mj * 128:(mj + 1) * 128], pd[:, k, c * CQ:(c + 1) * CQ],
                               k == 0, k == 1, [Wlb, pdb], grpb[gj])
                    act(v3(poolz[:, mo, :]), Gl, AF.Identity, reads=grpb[gj] + [tabb], pwrites=[poolzb],
                        scale=tab[:, TB_PS + mo:TB_PS + mo + 1])
    K.barrier()
    A.lo = mark_p

    sg0 = A.f32(TQ)
    sg1 = A.f32(TQ)
    t0 = A.f32(TQ)
    sg0b, sg1b, t0b = K.buf("sg0"), K.buf("sg1"), K.buf("t0")

    for blk in range(4):
        Wg0, Wg0b = load_w(w_in, 0, 16, OG + blk * 512, 512)
        Wg1, Wg1b = load_w(w_in, 0, 16, OG + D + blk * 512, 512)
        Wpo, Wao, Wpob = load_w2(blk)
        Waob = Wpob
        for mi in range(4):
            m = blk * 4 + mi
            msl = slice(mi * 128, (mi + 1) * 128)
            Msl = slice(m * 128, (m + 1) * 128)
            G = grp(0, 3, CQ)
            for k in range(KC):
                for c in range(3):
                    mm(G[:, c, :], Wg0[:, k, msl], h1x[:, k, 128 + c * CQ:128 + (c + 1) * CQ], k == 0, k == KC - 1,
                       [Wg0b, h1xb[k % 2]], grpb[0])
            dve_tt(v3(sg0[:, :]), G, v3(rsf[:, 128:128 + TQ]), ALU.mult, reads=grpb[0] + [rsfb], writes=[sg0b])
            act(sg0[:, :], sg0[:, :], AF.Sigmoid, reads=[sg0b, tabb], writes=[sg0b], bias=tab[:, TB_BG + m:TB_BG + m + 1], scale=1.0)
            G1 = grp(1, 3, CQ)
            for k in range(KC):
                for c in range(3):
                    mm(G1[:, c, :], Wg1[:, k, msl], h1x[:, k, 128 + c * CQ:128 + (c + 1) * CQ], k == 0, k == KC - 1,
                       [Wg1b, h1xb[k % 2]], grpb[1])
            dve_tt(v3(sg1[:, :]), G1, v3(rsf[:, 128:128 + TQ]), ALU.mult, reads=grpb[1] + [rsfb], writes=[sg1b])
            act(sg1[:, :], sg1[:, :], AF.Sigmoid, reads=[sg1b, tabb], writes=[sg1b],
                bias=tab[:, TB_BG + 16 + m:TB_BG + 16 + m + 1], scale=1.0)
            for k in range(8):
                for c in range(3):
                    mm(G[:, c, :], Wpo[:, k, msl], poolz[:, k, c * CQ:(c + 1) * CQ], k == 0, k == 7, [Wpob, poolzb], grpb[0])
            dve_tt(v3(t0[:, :]), G, v3(sg0[:, :]), ALU.mult, reads=grpb[0] + [sg0b], writes=[t0b])
            for k in range(4):
                for c in range(3):
                    mm(G1[:, c, :], Wao[:, k, msl], attnT[:, k, c * CQ:(c + 1) * CQ], k == 0, k == 3, [Waob, attnTb], grpb[1])
            dve_tt(v3(sg1[:, :]), G1, v3(sg1[:, :]), ALU.mult, reads=grpb[1] + [sg1b], writes=[sg1b])
            dve_tt(mixedT[:, m, :], t0[:, :], sg1[:, :], ALU.add, reads=[t0b, sg1b], pwrites=[mixedb])
    K.barrier()
    A.lo = mark_m
    A.hi = ARENA_WORDS

    x1T = A.f32(16 * TQ, top=True).rearrange("p (k t) -> p k t", k=16)
    x1b = [K.buf("x1_%d" % m) for m in range(16)]
    xr = [A.f32(TQ) for _ in range(2)]
    xrb = [K.buf("xr%d" % i, dma=True) for i in range(2)]
    for blk in range(4):
        W, Wb = load_w(w_out, 0, 16, blk * 512, 512)
        for mi in range(4):
            m = blk * 4 + mi
            gi = next_grp()
            G = grp(gi, 3, CQ)
            dma("sp", xr[m % 2], xT[m * 128:(m + 1) * 128, J0:J0 + TQ], xrb[m % 2], writes=[xrb[m % 2]])
            for k in range(KC):
                for c in range(3):
                    mm(G[:, c, :], W[:, k, mi * 128:(mi + 1) * 128], mixedT[:, k, c * CQ:(c + 1) * CQ], k == 0, k == KC - 1,
                       [Wb, mixedb], grpb[gi])
            dve_tt(v3(x1T[:, m, :]), G, v3(xr[m % 2][:, :]), ALU.add, reads=grpb[gi] + [xrb[m % 2]], writes=[x1b[m]])
    K.barrier()
    A.lo = mark_a

    h2T = A.bf16(16 * TQ).rearrange("p (k t) -> p k t", k=16)
    h2b = (K.buf("h2e"), K.buf("h2o"))
    NQF = 11
    prodT_flat = A.bf16(NQF * T)
    prodT = prodT_flat.rearrange("p (k t) -> p k t", k=NQF)
    prodb = K.buf("prodT")
    rs2 = rsblk[:, 0:TQ]
    rs2b = K.buf("rs2")
    sq2 = [prodT_flat[:, i * 4 * CQ:(i + 1) * 4 * CQ].rearrange("p (k t) -> p k t", k=4) for i in range(2)]
    sq2b = [K.buf("sq2_%d" % i) for i in range(2)]
    last_sq_read = None
    for c in range(3):
        cols = slice(c * CQ, (c + 1) * CQ)
        b6 = 6 + st["b6"]
        st["b6"] = 1 - st["b6"]
        pb = bank(b6)
        for q4 in range(4):
            i = (c * 4 + q4) % 2
            act(sq2[i], x1T[:, 4 * q4:4 * q4 + 4, cols], AF.Square, reads=x1b[4 * q4:4 * q4 + 4], writes=[sq2b[i]])
            for kk in range(4):
                k = 4 * q4 + kk
                last_sq_read = mm(pb[:, 0:CQ], ones[:, :], sq2[i][:, kk, :], k == 0, k == KC - 1, [sq2b[i], onesb], [bk[b6]])
        act(rs2[:, cols], pb[:, 0:CQ], AF.Sqrt, reads=[bk[b6], epsb], pwrites=[rs2b], bias=epsb_ap[:, 0:1], scale=1.0 / D)
        K.op("dve", lambda e, o=rs2[:, cols]: e.reciprocal(out=o, in_=o), reads=[rs2b], pwrites=[rs2b])
        for k in range(KC):
            gk = tab[:, TB_GFFN + k:TB_GFFN + k + 1]
            K.op("dve", lambda e, o=h2T[:, k, cols], i_=x1T[:, k, cols], s_=gk: e.tensor_scalar_mul(out=o, in0=i_, scalar1=s_),
                 reads=[x1b[k], tabb], pwrites=[h2b[k % 2]])

    ga = A.f32(T).rearrange("p (k t) -> p k t", k=1)
    gab = [K.buf("ga0"), K.buf("ga1")]
    preS = [A.f32(TQ)] * 2
    preSb = [K.buf("preS0")] * 2
    c0t = [A.f32(TQ) for _ in range(2)]
    c0b = [K.buf("c0_%d" % i) for i in range(2)]
    cvn = 0

    def conv_chunk(W, Wb, ci, f):
        nonlocal cvn
        gi = next_grp()
        G = grp(gi, 3, CQ)
        for k in range(KC):
            for c in range(3):
                mm(G[:, c, :], W[:, k, ci * 128:(ci + 1) * 128], h2T[:, k, c * CQ:(c + 1) * CQ], k == 0, k == KC - 1,
                   [Wb, h2b[k % 2]], grpb[gi])
        p = cvn % 2
        cvn += 1
        ps_, psb_ = preS[p], preSb[p]
        cc, ccb = c0t[p], c0b[p]
        dve_tt(v3(ps_[:, :]), G, v3(rs2), ALU.mult, reads=grpb[gi] + [rs2b], writes=[psb_])
        act(cc[:, :], ps_[:, :], AF.Identity, reads=[psb_, tabb], writes=[ccb],
            bias=tab[:, TB_CB + f:TB_CB + f + 1], scale=tab[:, TB_CW + 2 * 88 + f:TB_CW + 2 * 88 + f + 1])
        dve_stt(cc[:, 1:TQ], ps_[:, 0:TQ - 1], tab[:, TB_CW + 88 + f:TB_CW + 88 + f + 1], cc[:, 1:TQ], ALU.mult, ALU.add,
                reads=[psb_, ccb, tabb], writes=[ccb])
        dve_stt(cc[:, 2:TQ], ps_[:, 0:TQ - 2], tab[:, TB_CW + f:TB_CW + f + 1], cc[:, 2:TQ], ALU.mult, ALU.add,
                reads=[psb_, ccb, tabb], writes=[ccb])
        return cc, ccb

    def load_wab(f):
        dst, wb_ = load_flat(w_up[f * 128:(f + 1) * 128, :], 4096)
        return (dst[:, 0:2048].rearrange("p (k n) -> p k n", k=16),
                dst[:, 2048:4096].rearrange("p (k n) -> p k n", k=16), wb_)

    units = []
    for qq in range(4):
        for f in range(qq * NQF, (qq + 1) * NQF):
            units.append(("A", f))
            units.append(("B", f))
        units.append(("D", qq))
    early = set()
    for i in range(len(units) - 1):
        if units[i][0] == "D" and units[i + 1][0] == "A":
            units[i], units[i + 1] = units[i + 1], units[i]
            early.add(units[i][1])
    wabs = {}
    for kind, v in units:
        if kind == "A":
            f = v
            if f in early:
                Wa, Wab = load_w(w_up, 0, 16, f * 128, 128)
            else:
                wabs[f] = load_wab(f)
                Wa, Wbk, Wab = wabs[f]
            cc, ccb = conv_chunk(Wa, Wab, 0, f)
            act(ga[:, 0, :], cc[:, 2:TQ], AF.Gelu, reads=[ccb], writes=[gab[0]])
        elif kind == "B":
            f = v
            if f in early:
                Wbk, Wab = load_w(w_up, 0, 16, DFF + f * 128, 128)
            else:
                Wa, Wbk, Wab = wabs.pop(f)
            f0 = (f // NQF) * NQF
            cc, ccb = conv_chunk(Wbk, Wab, 0, FC + f)
            dve_tt(prodT[:, f - f0, :], cc[:, 2:TQ], ga[:, 0, :], ALU.mult, reads=[ccb, gab[0]], pwrites=[prodb],
                   extra=[last_sq_read] if f == 0 else ())
        else:
            qq = v
            f0 = qq * NQF
            for blk in range(4):
                Wd, Wdb = load_w(w_down, f0, NQF, blk * 512, 512)
                for mi in range(4):
                    m = blk * 4 + mi
                    gi = next_grp()
                    G = grp(gi, 2, 512)
                    for k in range(NQF):
                        for c in range(2):
                            mm(G[:, c, :], Wd[:, k, mi * 128:(mi + 1) * 128], prodT[:, k, c * 512:(c + 1) * 512],
                               k == 0, k == NQF - 1, [Wdb, prodb], grpb[gi])
                    xv = x1T[:, m, 2:TQ].rearrange("p (c t) -> p c t", c=2)
                    dve_tt(xv, G, xv, ALU.add, reads=grpb[gi] + [x1b[m]], writes=[x1b[m]])
    K.barrier()
    A.lo = mark_a

    sqf = A.bf16(16 * 256).rearrange("p (k t) -> p k t", k=16)
    sqfb = K.buf("sqf")
    ost = [A.f32(16 * 256).rearrange("p (k t) -> p k t", k=16) for _ in range(2)]
    ostb = [K.buf("ost%d" % i, dma=True) for i in range(2)]
    for c in range(4):
        xv = x1T[:, :, 2 + c * 256:2 + (c + 1) * 256]
        norm_core(xv, x1b, 256, TB_GF, ost[c % 2], ostb[c % 2], sqf, sqfb)
        dma("sp", outT.rearrange("(k p) t -> p k t", p=128)[:, :, c * 256:(c + 1) * 256], ost[c % 2], ostb[c % 2],
            reads=[ostb[c % 2]])
    K.barrier()
    K.op("sp", None)
    K.emit(nc)
    return nc, kvcols


_CACHE = {}


def _host_tables(inputs, kvcols):
    def colmaj(v, nchunk):
        return np.ascontiguousarray(np.asarray(v, np.float32).reshape(nchunk, 128).T)

    base = np.zeros((128, NTAB), np.float32)
    base[:, TB_GMIX:TB_GMIX + 16] = colmaj(inputs["g_mix"][0], 16)
    base[:, TB_BG:TB_BG + 32] = colmaj(inputs["b_gate"][0], 32)
    base[:, TB_PS:TB_PS + 8] = colmaj(inputs["pool_scale"][0], 8)
    base[:, TB_GFFN:TB_GFFN + 16] = colmaj(inputs["g_ffn"][0], 16)
    cw = np.asarray(inputs["conv_w"][0], np.float32)
    for i in range(3):
        base[:, TB_CW + i * 88:TB_CW + (i + 1) * 88] = colmaj(cw[i], 88)
    base[:, TB_CB:TB_CB + 88] = colmaj(inputs["conv_b"][0], 88)
    base[:, TB_GF:TB_GF + 16] = colmaj(inputs["g_final"], 16)
    tabs = []
    for c in range(NCORES):
        t = base.copy()
        s0 = c * T
        for wi, w in enumerate(POOLW):
            for col in range(18):
                tt = s0 - 2 + col
                t[:, TB_CORR + wi * 18 + col] = (w / min(tt + 1, w)) if tt >= 0 else 1.0
        for (g, start), (idx, n) in kvcols.items():
            d = DIL[g]
            i = np.arange(128)
            tt = s0 - 2 - J0 + (J0 - HALO[g]) + start + d * i
            t[:, TB_KV + idx] = np.where(tt >= 0, 0.0, NEG)
        tabs.append(t)
    masks = np.zeros((128, 12, 256), np.float32)
    kk = np.arange(128)[:, None]
    ii = np.arange(128)[None, :]
    for hh in range(12):
        g = hh // 4
        d = DIL[g]
        slope = np.float32(2.0 ** (-8.0 * (hh + 1) / 12))
        jp = ii + 128 - kk
        masks[:, hh, 0:128] = np.where(kk >= ii, -slope * (jp * d).astype(np.float32), NEG)
        jc = ii - kk
        masks[:, hh, 128:256] = np.where(kk <= ii, -slope * (jc * d).astype(np.float32), NEG)
    return tabs, masks.reshape(128, 12 * 256)


def kernel(**inputs):
    if "prog" not in _CACHE:
        _CACHE["prog"] = build_program()
    nc, kvcols = _CACHE["prog"]
    x = np.asarray(inputs["x"], np.float32)[0]
    S = x.shape[0]
    xTfull = np.zeros((D, J0 + 2 + S), np.float32)
    xTfull[:, J0 + 2:] = x.T
    tabs, masks = _host_tables(inputs, kvcols)
    ident = np.eye(128, dtype=np.float32)
    def tile_blocks(W, nk, ncols):
        W = np.asarray(W, np.float32)
        nb = W.shape[1] // ncols
        return np.ascontiguousarray(W.reshape(nk, 128, nb, ncols).transpose(2, 1, 0, 3)).reshape(nb * 128, nk * ncols)

    w_up_full = np.asarray(inputs["w_up"], np.float32)[0]
    w_down_full = np.asarray(inputs["w_down"], np.float32)[0]
    wplf = np.asarray(inputs["w_pool_lin"], np.float32)[0]
    shared = {
        "w_in": tile_blocks(np.asarray(inputs["w_in"], np.float32)[0], 16, 512),
        "wpl": np.concatenate([tile_blocks(wplf[gi], 2, 256) for gi in range(4)], axis=0),
        "wpa": np.concatenate([tile_blocks(np.asarray(inputs["w_pool_out"], np.float32)[0], 8, 512),
                               tile_blocks(np.asarray(inputs["w_attn_out"], np.float32)[0], 4, 512)], axis=1),
        "w_out": tile_blocks(np.asarray(inputs["w_out"], np.float32)[0], 16, 512),
        "w_up": np.concatenate([tile_blocks(w_up_full[:, :DFF], 16, 128), tile_blocks(w_up_full[:, DFF:], 16, 128)], axis=1),
        "w_down": np.concatenate([tile_blocks(w_down_full[q * 1408:(q + 1) * 1408], 11, 512) for q in range(4)], axis=0),
        "masks": masks,
        "ident": ident,
    }
    in_maps = []
    for c in range(NCORES):
        m = dict(shared)
        m["xT"] = np.ascontiguousarray(xTfull[:, c * T:c * T + NT])
        m["tab"] = tabs[c]
        in_maps.append(m)
    res = run_bass_kernel_spmd(nc, in_maps, core_ids=list(range(NCORES)))
    out = np.empty((1, S, D), np.float32)
    for c in range(NCORES):
        out[0, c * T:(c + 1) * T, :] = np.asarray(res.results[c]["outT"]).T
    return out
```

```python
import numpy as np
import concourse.bass as bass
import concourse.mybir as mybir
from concourse.bass_utils import run_bass_kernel_spmd

F32 = mybir.dt.float32
BF16 = mybir.dt.bfloat16
AF = mybir.ActivationFunctionType
ALU = mybir.AluOpType

NCORES = 8
D = 2048
KC = 16
T = 1024
TQ = 1026
J0 = 2048
NT = J0 + TQ
UH = 15
TU = TQ + UH
OQ, OK_, OV, OG = 1024, 2560, 4096, 5632
INW = 9728
DFF = 5632
FC = 44
DIL = [1, 4, 16]
HALO = [128, 512, 2048]
POOLW = [2, 4, 8, 16]
NEG = -30000.0
EPS = 1e-6
NKV = 96
NX1 = 128 + TQ
TB_GMIX, TB_BG, TB_PS, TB_GFFN, TB_CW, TB_CB, TB_GF, TB_CORR, TB_KV = 0, 16, 48, 56, 72, 336, 424, 440, 512
NTAB = TB_KV + NKV
ARENA_WORDS = 52600
ENGS = ("pe", "act", "dve", "pool", "sp")


class Buf:
    __slots__ = ("name", "lws", "rd", "drd", "sem", "dcount", "dma", "keep")

    def __init__(self, name, dma=False):
        self.name = name
        self.keep = False
        self.lws = []
        self.rd = {}
        self.drd = []
        self.sem = None
        self.dcount = 0
        self.dma = dma


class Op:
    __slots__ = ("eng", "fn", "deps", "is_dma", "dbuf", "event", "sig")

    def __init__(self, eng, fn, is_dma, dbuf):
        self.eng = eng
        self.fn = fn
        self.deps = set()
        self.is_dma = is_dma
        self.dbuf = dbuf
        self.event = None
        self.sig = False


class Kern:
    def __init__(self):
        self.streams = {e: [] for e in ENGS}
        self.bufs = []
        self.pending = {e: [] for e in ENGS}
        self.dma_since = []
        self.last_barrier = []

    def buf(self, name, dma=False):
        b = Buf(name, dma)
        self.bufs.append(b)
        return b

    def op(self, eng, fn, reads=(), writes=(), pwrites=(), dma_buf=None, extra=()):
        o = Op(eng, fn, dma_buf is not None, dma_buf)
        deps = o.deps
        deps.update(extra)
        if self.pending[eng]:
            deps.update(self.pending[eng])
            self.pending[eng] = []
        for b in reads:
            deps.update(b.lws)
        for b in writes:
            deps.update(b.lws)
            deps.update(b.rd.values())
            deps.update(b.drd)
        for b in pwrites:
            deps.update(b.rd.values())
            deps.update(b.drd)
        if not o.is_dma and eng == "pe":
            o.deps = deps = {d for d in deps if d.is_dma or d.eng != "pe"}
        deps.discard(o)
        for b in reads:
            if o.is_dma:
                b.drd.append(o)
            else:
                b.rd[eng] = o
        for b in writes:
            b.lws = [o]
            b.rd = {}
            b.drd = []
        for b in pwrites:
            b.lws.append(o)
        self.streams[eng].append(o)
        if o.is_dma:
            self.dma_since.append(o)
        return o

    def barrier(self):
        last = [s[-1] for e, s in self.streams.items() if s and e != "pool"]
        allops = last + [o for o in self.dma_since if not o.dbuf.keep]
        for e in ENGS:
            if e != "pool":
                self.pending[e] = list(allops)
        self.last_barrier = list(allops)
        self.dma_since = [o for o in self.dma_since if o.dbuf.keep]
        for b in self.bufs:
            if b.keep:
                continue
            b.lws = []
            b.rd = {}
            b.drd = []

    def emit(self, nc):
        for ops in self.streams.values():
            for o in ops:
                for d in o.deps:
                    d.sig = True
        engsem = {e: nc.alloc_semaphore("s_" + e) for e in ENGS}
        for b in self.bufs:
            if b.dma:
                b.sem = nc.alloc_semaphore("d_" + b.name)
        for e in ENGS:
            tick = 0
            for o in self.streams[e]:
                if o.is_dma:
                    b = o.dbuf
                    b.dcount += 1
                    o.event = (b.sem, 16 * b.dcount)
                elif o.sig:
                    tick += 1
                    o.event = (engsem[e], tick)
        streams = self.streams

        def run(e, h):
            seen = {}
            for o in streams[e]:
                need = {}
                for d in o.deps:
                    s, v = d.event
                    if need.get(s, 0) < v:
                        need[s] = v
                for s, v in need.items():
                    if seen.get(s, 0) < v:
                        h.wait_ge(s, v)
                        seen[s] = v
                if o.fn is not None:
                    ins = o.fn(h)
                    if o.is_dma:
                        ins.then_inc(o.event[0], 16)
                    elif o.sig:
                        ins.then_inc(o.event[0], 1)

        with nc.Block() as block:
            @block.tensor
            def _(h):
                run("pe", h)

            @block.scalar
            def _(h):
                run("act", h)

            @block.vector
            def _(h):
                run("dve", h)

            @block.gpsimd
            def _(h):
                run("pool", h)

            @block.sync
            def _(h):
                run("sp", h)


class Arena:
    def __init__(self, ap, nwords):
        self.ap = ap
        self.lo = 0
        self.hi = nwords

    def _take(self, nbytes, top):
        nw = (nbytes + 3) // 4
        nw = (nw + 7) // 8 * 8
        if top:
            self.hi -= nw
            off = self.hi
        else:
            off = self.lo
            self.lo += nw
        assert self.lo <= self.hi, ("SBUF arena overflow", self.lo, self.hi)
        return off, nw

    def f32(self, cols, top=False):
        off, nw = self._take(cols * 4, top)
        return self.ap[:, off:off + cols]

    def bf16(self, cols, top=False):
        assert cols % 2 == 0
        off, nw = self._take(cols * 2, top)
        return self.ap[:, off:off + cols // 2].bitcast(BF16)


def split_mult(start, total, maxc, mult):
    assert total % mult == 0
    units = total // mult
    n = -(-total // (maxc // mult * mult))
    base, rem = divmod(units, n)
    out = []
    s = start
    for i in range(n):
        c = (base + (1 if i < rem else 0)) * mult
        out.append((s, c))
        s += c
    return out


def split_chunks(start, total, maxc):
    n = -(-total // maxc)
    base, rem = divmod(total, n)
    out = []
    s = start
    for i in range(n):
        c = base + (1 if i < rem else 0)
        out.append((s, c))
        s += c
    return out


def build_program():
    nc = bass.Bass("TRN2", target_bir_lowering=False)
    dr = {}

    def din(name, shape):
        dr[name] = nc.dram_tensor(name, list(shape), F32, kind="ExternalInput").ap()
        return dr[name]

    xT = din("xT", (D, NT))
    w_in = din("w_in", (19 * 128, 16 * 512))
    wpl = din("wpl", (4 * 128, 2 * 256))
    wpa = din("wpa", (4 * 128, 12 * 512))
    w_out = din("w_out", (4 * 128, 16 * 512))
    w_up = din("w_up", (FC * 128, 2 * 16 * 128))
    w_down = din("w_down", (16 * 128, 11 * 512))
    tabd = din("tab", (128, NTAB))
    masksd = din("masks", (128, 12 * 256))
    identd = din("ident", (128, 128))
    outT = nc.dram_tensor("outT", [D, T], F32, kind="ExternalOutput").ap()

    K = Kern()
    kvcols = {}

    arena_t = nc.sbuf_tensor("arena", [128, ARENA_WORDS], F32).__enter__()
    psA = nc.psum_tensor("psA", [128, 3, 512], F32).__enter__()
    psB = nc.psum_tensor("psB", [128, 3, 512], F32).__enter__()
    psC = nc.psum_tensor("psC", [128, 512], F32).__enter__()
    psD = nc.psum_tensor("psD", [128, 512], F32).__enter__()
    A = Arena(arena_t[:, :], ARENA_WORDS)

    def bank(i):
        if i < 3:
            return psA[:, i, :]
        if i < 6:
            return psB[:, i - 3, :]
        return psC[:, :] if i == 6 else psD[:, :]

    def grp(gi, nb, w):
        return (psA if gi == 0 else psB)[:, 0:nb, 0:w]

    bk = [K.buf("bank%d" % i) for i in range(8)]
    grpb = [[bk[0], bk[1], bk[2]], [bk[3], bk[4], bk[5]]]
    st = {"g": 0, "sb": 0, "w": 0, "b6": 0, "ev": 0}

    def mm(out, lhsT, rhs, start, stop, reads, writes):
        return K.op("pe", lambda e, o=out, l=lhsT, r=rhs, s=start, t=stop: e.matmul(o, lhsT=l, rhs=r, start=s, stop=t),
             reads=reads, writes=writes)

    def act(out, in_, func, reads, writes=(), pwrites=(), bias=None, scale=None):
        kw = {}
        if bias is not None:
            kw["bias"] = bias
        if scale is not None:
            kw["scale"] = scale
        K.op("act", lambda e, o=out, i=in_, f=func, kw=kw: e.activation(out=o, in_=i, func=f, **kw),
             reads=reads, writes=writes, pwrites=pwrites)

    def dve_tt(out, in0, in1, op, reads, writes=(), pwrites=(), extra=()):
        return K.op("dve", lambda e, o=out, a=in0, b=in1, p=op: e.tensor_tensor(out=o, in0=a, in1=b, op=p),
             reads=reads, writes=writes, pwrites=pwrites, extra=extra)

    def dve_stt(out, in0, scalar, in1, op0, op1, reads, writes=(), pwrites=()):
        K.op("dve", lambda e, o=out, a=in0, s=scalar, b=in1, p0=op0, p1=op1:
             e.scalar_tensor_tensor(out=o, in0=a, scalar=s, in1=b, op0=p0, op1=p1),
             reads=reads, writes=writes, pwrites=pwrites)

    def dve_copy(out, in_, reads, writes=(), pwrites=()):
        K.op("dve", lambda e, o=out, i=in_: e.tensor_copy(out=o, in_=i), reads=reads, writes=writes, pwrites=pwrites)

    def dma(eng, out, in_, buf, reads=(), writes=()):
        K.op(eng, lambda e, o=out, i=in_: e.dma_start(out=o, in_=i), reads=reads, writes=writes, dma_buf=buf)

    tab = A.f32(NTAB)
    tabb = K.buf("tab", dma=True)
    ones = A.bf16(128)
    onesb = K.buf("ones")
    ident = A.bf16(128)
    identb = K.buf("ident", dma=True)
    epsb_ap = A.f32(8)
    epsb = K.buf("eps")
    rsblk = A.f32(1040)
    rs = [rsblk[:, 0:520], rsblk[:, 520:1040]]
    rsb = [K.buf("rs0"), K.buf("rs1")]
    WSLOT = 8192
    wslot = [A.bf16(WSLOT) for _ in range(3)]
    wbuf = [K.buf("w%d" % i, dma=True) for i in range(3)]
    for wb_ in wbuf:
        wb_.keep = True
    st["nslots"] = 3

    dma("sp", tab, tabd, tabb, writes=[tabb])
    dma("pool", ident, identd, identb, writes=[identb])
    K.op("dve", lambda e: e.memset(ones, 1.0), writes=[onesb])
    K.op("dve", lambda e: e.memset(epsb_ap, EPS), writes=[epsb])

    def slot_extra(i):
        if i == 3 and not st.get("w3used"):
            st["w3used"] = True
            return list(st["w3deps"])
        return ()

    def load_flat(src2d, n):
        i = st["w"]
        st["w"] = (i + 1) % st["nslots"]
        assert n <= WSLOT
        dst = wslot[i][:, 0:n]
        K.op("pool", lambda e, o=dst, s_=src2d: e.dma_start(out=o, in_=s_), writes=[wbuf[i]], dma_buf=wbuf[i],
             extra=slot_extra(i))
        return dst, wbuf[i]

    def load_w(W, k0, nk, c0, ncols):
        if W is w_in or W is w_out:
            assert k0 == 0 and nk == 16 and ncols == 512 and c0 % 512 == 0
            b_ = c0 // 512
            dst, wb_ = load_flat(W[b_ * 128:(b_ + 1) * 128, :], 8192)
        elif W is wpl:
            assert nk == 2 and ncols == 256 and c0 == 0
            gi_ = k0 // 2
            dst, wb_ = load_flat(W[gi_ * 128:(gi_ + 1) * 128, :], 512)
        elif W is w_down:
            assert nk == 11 and ncols == 512
            rb_ = (k0 // 11) * 4 + c0 // 512
            dst, wb_ = load_flat(W[rb_ * 128:(rb_ + 1) * 128, :], 11 * 512)
        elif W is w_up:
            assert k0 == 0 and nk == 16 and ncols == 128
            half_ = 0 if c0 < DFF else 1
            f_ = (c0 - half_ * DFF) // 128
            dst, wb_ = load_flat(W[f_ * 128:(f_ + 1) * 128, half_ * 2048:(half_ + 1) * 2048], 2048)
        else:
            raise AssertionError("unknown weight")
        return dst.rearrange("p (k n) -> p k n", k=nk), wb_

    def load_w2(blk):
        dst, wb_ = load_flat(wpa[blk * 128:(blk + 1) * 128, :], 12 * 512)
        return (dst[:, 0:4096].rearrange("p (k n) -> p k n", k=8),
                dst[:, 4096:6144].rearrange("p (k n) -> p k n", k=4), wb_)

    def next_grp():
        g = st["g"]
        st["g"] = 1 - g
        return g

    def next_bank6():
        b = st["sb"]
        st["sb"] = (b + 1) % 6
        return b

    def norm_core(xap, xbufs, n, gcol, hout, hbuf, sqap, sqbuf):
        act(sqap, xap, AF.Square, reads=xbufs, pwrites=[sqbuf])
        b6 = 6 + st["b6"]
        st["b6"] = 1 - st["b6"]
        pb = bank(b6)
        for k in range(KC):
            mm(pb[:, 0:n], ones[:, :], sqap[:, k, :], k == 0, k == KC - 1, [sqbuf, onesb], [bk[b6]])
        r = rs[b6 - 6]
        rb = rsb[b6 - 6]
        act(r[:, 0:n], pb[:, 0:n], AF.Sqrt, reads=[bk[b6], epsb], writes=[rb], bias=epsb_ap[:, 0:1], scale=1.0 / D)
        K.op("dve", lambda e, o=r[:, 0:n]: e.reciprocal(out=o, in_=o), reads=[rb], writes=[rb])
        for k in range(KC):
            dve_stt(hout[:, k, :], xap[:, k, :], tab[:, gcol + k:gcol + k + 1], r[:, 0:n], ALU.mult, ALU.mult,
                    reads=list(xbufs) + [rb, tabb] + ([sqbuf] if sqbuf is hbuf else []), pwrites=[hbuf])

    def evac_copy(out, in_, reads, pwrites):
        st["ev"] ^= 1
        if st["ev"]:
            act(out, in_, AF.Copy, reads=reads, pwrites=pwrites)
        else:
            dve_copy(out, in_, reads=reads, pwrites=pwrites)

    attnTb = K.buf("attnT")
    mark_a = A.lo
    acc = A.f32(2 * 4 * TQ).rearrange("p (a h t) -> p a h t", a=2, h=4)
    accb = K.buf("acc")
    K.op("dve", lambda e: e.memset(acc, 0.0), writes=[accb])
    mark_g = A.lo
    sm_scale = 1.0 / np.sqrt(128.0)
    xTv = xT.rearrange("(k p) t -> p k t", p=128)

    for g in (2, 1, 0):
        A.lo = mark_g
        d = DIL[g]
        H = HALO[g]
        NK = H + TQ
        mk = A.f32(4 * 256).rearrange("p (h c) -> p h c", h=4)
        mkb = K.buf("mk%d" % g, dma=True)
        CH = 192
        xs = [A.f32(16 * CH).rearrange("p (k t) -> p k t", k=16) for _ in range(2)]
        xsb = [K.buf("xs%d_%d" % (g, i), dma=True) for i in range(2)]
        if g == 0:
            h1x = A.bf16(16 * NX1, top=True).rearrange("p (k t) -> p k t", k=16)
            rsf = A.f32(NX1 + 6, top=True)
            h1xb = (K.buf("h1xe"), K.buf("h1xo"))
            rsfb = K.buf("rsf")
            hc = hcb = None
        else:
            hc = [A.bf16(16 * CH).rearrange("p (k t) -> p k t", k=16) for _ in range(2)]
            hcb = [(K.buf("hce%d_%d" % (g, i)), K.buf("hco%d_%d" % (g, i))) for i in range(2)]
        sqs = [A.bf16(4 * CH).rearrange("p (k t) -> p k t", k=4) for _ in range(8)]
        sqsb = [K.buf("sq%d_%d" % (g, i)) for i in range(8)]
        Lk = -(-NK // d)
        Lq = -(-TQ // d)
        kT = A.bf16(4 * d * Lk).rearrange("p (h t) -> p h t", h=4)
        vT = A.bf16(4 * d * Lk).rearrange("p (h t) -> p h t", h=4)
        qT = A.bf16(4 * d * Lq).rearrange("p (h t) -> p h t", h=4)
        kTb, vTb, qTb = K.buf("kT%d" % g), K.buf("vT%d" % g), K.buf("qT%d" % g)
        NS = 4
        Ssb = [A.f32(256) for _ in range(NS)]
        Ssbb = [K.buf("Ssb%d_%d" % (g, i)) for i in range(NS)]
        PT = [A.bf16(256) for _ in range(NS)]
        PTb = [K.buf("PT%d_%d" % (g, i)) for i in range(NS)]
        Vsb = [A.bf16(256) for _ in range(NS)]
        Vsbb = [K.buf("Vsb%d_%d" % (g, i)) for i in range(NS)]

        dma("sp", mk, masksd[:, g * 1024:(g + 1) * 1024].rearrange("p (h c) -> p h c", h=4), mkb, writes=[mkb])
        Wk, Wkb = load_w(w_in, 0, 16, OK_ + g * 512, 512)
        Wv, Wvb = load_w(w_in, 0, 16, OV + g * 512, 512)
        Wq, Wqb = load_w(w_in, 0, 16, OQ + g * 512, 512)

        own_ch = split_mult(J0, TQ - 2, CH - 2, d)
        own_ch[-1] = (own_ch[-1][0], own_ch[-1][1] + 2)
        chunks = split_mult(J0 - H, H, CH, d) + own_ch
        sqn = [0]

        def stage1(ci):
            j0, n = chunks[ci]
            x_, xb_ = xs[ci % 2], xsb[ci % 2]
            dma("sp", x_[:, :, 0:n], xTv[:, :, j0:j0 + n], xb_, writes=[xb_])
            if g == 0:
                jj_ = j0 - (J0 - H)
                h = h1x[:, :, jj_:jj_ + n]
                hb = h1xb
            else:
                h = hc[ci % 2]
                hb = hcb[ci % 2]
            for k in range(KC):
                gk = tab[:, TB_GMIX + k:TB_GMIX + k + 1]
                first = (k < 2) and g != 0
                hbk = hb[k % 2]
                if k % 2 == 0:
                    K.op("dve", lambda e, o=h[:, k, 0:n], i=x_[:, k, 0:n], s_=gk: e.tensor_scalar_mul(out=o, in0=i, scalar1=s_),
                         reads=[xb_, tabb], writes=[hbk] if first else (), pwrites=() if first else [hbk])
                else:
                    act(h[:, k, 0:n], x_[:, k, 0:n], AF.Identity, reads=[xb_, tabb], writes=[hbk] if first else (),
                        pwrites=() if first else [hbk], scale=gk)
            sl = []
            for q4 in range(4):
                si = 4 * (ci % 2) + q4
                act(sqs[si][:, :, 0:n], x_[:, 4 * q4:4 * q4 + 4, 0:n], AF.Square, reads=[xb_], writes=[sqsb[si]])
                sl.append(si)
            return sl

        def stage2(ci, sl):
            j0, n = chunks[ci]
            own = j0 >= J0
            jj = j0 - (J0 - H)
            if g == 0:
                h = h1x[:, :, jj:jj + n]
                hb = h1xb
            else:
                h = hc[ci % 2]
                hb = hcb[ci % 2]
            b6 = 6 + st["b6"]
            st["b6"] = 1 - st["b6"]
            pb = bank(b6)
            for q4 in range(4):
                for kk in range(4):
                    k = 4 * q4 + kk
                    mm(pb[:, 0:n], ones[:, :], sqs[sl[q4]][:, kk, 0:n], k == 0, k == KC - 1, [sqsb[sl[q4]], onesb], [bk[b6]])
            if g == 0:
                r = rsf[:, jj:jj + n]
                rb = rsfb
                act(r[:, 0:n], pb[:, 0:n], AF.Sqrt, reads=[bk[b6], epsb], pwrites=[rb], bias=epsb_ap[:, 0:1], scale=1.0 / D)
                K.op("dve", lambda e, o=r[:, 0:n]: e.reciprocal(out=o, in_=o), reads=[rb], pwrites=[rb])
            else:
                r = rs[b6 - 6]
                rb = rsb[b6 - 6]
                act(r[:, 0:n], pb[:, 0:n], AF.Sqrt, reads=[bk[b6], epsb], writes=[rb], bias=epsb_ap[:, 0:1], scale=1.0 / D)
                K.op("dve", lambda e, o=r[:, 0:n]: e.reciprocal(out=o, in_=o), reads=[rb], writes=[rb])
            targets = [(Wk, Wkb, kT, kTb, jj, Lk), (Wv, Wvb, vT, vTb, jj, Lk)]
            if own:
                targets.append((Wq, Wqb, qT, qTb, j0 - J0, Lq))
            nmain = n - (n % d)
            for (W, Wb, dst, dstb, c0, Ld) in targets:
                for m in range(4):
                    b = next_bank6()
                    pbk = bank(b)
                    for k in range(KC):
                        mm(pbk[:, 0:n], W[:, k, m * 128:(m + 1) * 128], h[:, k, 0:n], k == 0, k == KC - 1,
                           [Wb, hb[k % 2]], [bk[b]])
                    if d == 1:
                        dve_tt(dst[:, m, c0:c0 + n], pbk[:, 0:n], r[:, 0:n], ALU.mult, reads=[bk[b], rb], pwrites=[dstb])
                    else:
                        assert c0 % d == 0
                        ov = dst[:, m, :].rearrange("p (r l) -> p r l", r=d)[:, :, c0 // d:c0 // d + nmain // d]
                        dve_tt(ov, pbk[:, 0:nmain].rearrange("p (l r) -> p r l", r=d),
                               r[:, 0:nmain].rearrange("p (l r) -> p r l", r=d), ALU.mult,
                               reads=[bk[b], rb], pwrites=[dstb])
                        if n > nmain:
                            assert n - nmain == 2 and (c0 + nmain) // d == Ld - 1
                            dve_tt(dst[:, m, Ld - 1:2 * Ld:Ld], pbk[:, nmain:n], r[:, nmain:n], ALU.mult,
                                   reads=[bk[b], rb], pwrites=[dstb])

        sls = {0: stage1(0)}
        for ci in range(len(chunks)):
            if ci + 1 < len(chunks):
                sls[ci + 1] = stage1(ci + 1)
            stage2(ci, sls[ci])
        K.barrier()

        def kvcol(start, n):
            key = (g, start)
            if key not in kvcols:
                kvcols[key] = (len(kvcols), n)
            assert kvcols[key][0] < NKV
            return TB_KV + kvcols[key][0]

        tiles = []
        for hh in range(4):
            for rho in range(d):
                M = -(-(TQ - rho) // d)
                for m0 in range(0, M, 128):
                    tiles.append((hh, rho, m0, min(128, M - m0)))

        def geom(t):
            hh, rho, m0, nq = t
            q0 = rho + d * m0
            asl = slice(q0, q0 + d * (nq - 1) + 1, d)
            qsl = slice(rho * Lq + m0, rho * Lq + m0 + nq)
            sp_ = H + rho + d * (m0 - 128)
            sc_ = H + rho + d * m0
            psl = slice(rho * Lk + m0, rho * Lk + m0 + 128)
            csl = slice(rho * Lk + 128 + m0, rho * Lk + 128 + m0 + nq)
            return hh, nq, qsl, sp_, sc_, psl, csl, asl

        def tstage1(ti):
            hh, nq, qsl, sp_, sc_, psl, csl, asl = geom(tiles[ti])
            s_ = ti % NS
            Sb = bk[2 * s_]
            sv = (ti + NS - 1) % NS
            Vb = bk[2 * sv + 1]
            S = bank(2 * s_)[:, 0:256]
            Vp = bank(2 * sv + 1)[:, 256:512]
            qv = qT[:, hh, qsl]
            mm(S[:, 0:nq], kT[:, hh, psl], qv, True, True, [kTb, qTb], [Sb])
            mm(S[0:nq, 128:128 + nq], kT[:, hh, csl], qv, True, True, [kTb, qTb], [Sb])
            mm(Vp[:, 0:128], vT[:, hh, psl], ident[:, :], True, True, [vTb, identb], [Vb])
            mm(Vp[0:nq, 128:256], vT[:, hh, csl], ident[:, :], True, True, [vTb, identb], [Vb])
            ss, ssb_ = Ssb[s_], Ssbb[s_]
            pt, ptb = PT[s_], PTb[s_]
            vs, vsb_ = Vsb[s_], Vsbb[s_]
            if nq == 128:
                dve_stt(ss[:, 0:256], S[:, 0:256], sm_scale, mk[:, hh, 0:256], ALU.mult, ALU.add,
                        reads=[Sb, mkb], writes=[ssb_])
                act(vs[:, 0:256], Vp[:, 0:256], AF.Copy, reads=[Vb], writes=[vsb_])
            else:
                dve_stt(ss[:, 0:nq], S[:, 0:nq], sm_scale, mk[:, hh, 0:nq], ALU.mult, ALU.add,
                        reads=[Sb, mkb], writes=[ssb_])
                dve_stt(ss[0:nq, 128:128 + nq], S[0:nq, 128:128 + nq], sm_scale, mk[0:nq, hh, 128:128 + nq],
                        ALU.mult, ALU.add, reads=[Sb, mkb], pwrites=[ssb_])
                act(vs[:, 0:128], Vp[:, 0:128], AF.Copy, reads=[Vb], writes=[vsb_])
                act(vs[0:nq, 128:256], Vp[0:nq, 128:256], AF.Copy, reads=[Vb], pwrites=[vsb_])
            cp = kvcol(sp_, 128)
            cc = kvcol(sc_, nq)
            act(pt[:, 0:nq], ss[:, 0:nq], AF.Exp, reads=[ssb_, tabb], writes=[ptb], bias=tab[:, cp:cp + 1], scale=1.0)
            act(pt[0:nq, 128:128 + nq], ss[0:nq, 128:128 + nq], AF.Exp, reads=[ssb_, tabb], pwrites=[ptb],
                bias=tab[0:nq, cc:cc + 1], scale=1.0)

        def tstage2(ti):
            hh, nq, qsl, sp_, sc_, psl, csl, asl = geom(tiles[ti])
            s_ = ti % NS
            OLb = bk[2 * s_ + 1]
            OL = bank(2 * s_ + 1)[:, 0:256]
            pt, ptb = PT[s_], PTb[s_]
            vs, vsb_ = Vsb[s_], Vsbb[s_]
            mm(OL[:, 0:nq], vs[:, 0:128], pt[:, 0:nq], True, False, [vsb_, ptb], [OLb])
            mm(OL[:, 0:nq], vs[0:nq, 128:256], pt[0:nq, 128:128 + nq], False, True, [vsb_, ptb], [OLb])
            mm(OL[:, 128:128 + nq], ones[:, :], pt[:, 0:nq], True, False, [onesb, ptb], [OLb])
            mm(OL[:, 128:128 + nq], ones[0:nq, :], pt[0:nq, 128:128 + nq], False, True, [onesb, ptb], [OLb])
            accv = acc[:, :, hh, asl]
            olv = OL.rearrange("p (a c) -> p a c", a=2)[:, :, 0:nq]
            dve_tt(accv, olv, accv, ALU.add, reads=[OLb, accb], pwrites=[accb])

        DEPTH = NS - 1
        for ti in range(len(tiles) + DEPTH):
            if ti < len(tiles):
                tstage1(ti)
            if ti >= DEPTH:
                tstage2(ti - DEPTH)
        K.barrier()

    A.lo = mark_g
    nrm_ops = []
    attnT = A.bf16(4 * TQ, top=True).rearrange("p (h t) -> p h t", h=4)
    for hh in range(4):
        K.op("dve", lambda e, o=acc[:, 1, hh, :]: e.tensor_scalar_max(out=o, in0=o, scalar1=1e-30), reads=[accb], pwrites=[accb])
        act(acc[:, 1, hh, :], acc[:, 1, hh, :], AF.Ln, reads=[accb], pwrites=[accb])
        act(acc[:, 1, hh, :], acc[:, 1, hh, :], AF.Exp, reads=[accb], pwrites=[accb], scale=-1.0)
        nrm_ops.append(dve_tt(attnT[:, hh, :], acc[:, 0, hh, :], acc[:, 1, hh, :], ALU.mult, reads=[accb], pwrites=[attnTb]))
    A.lo = mark_a
    wslot.append(A.bf16(WSLOT))
    wb3 = K.buf("w3", dma=True)
    wb3.keep = True
    wbuf.append(wb3)
    st["w3deps"] = list(K.last_barrier) + nrm_ops
    st["nslots"] = 4
    mark_a = A.lo

    mixedT = A.bf16(16 * TQ).rearrange("p (k t) -> p k t", k=16)
    mixedb = K.buf("mixedT")
    mark_m = A.lo
    poolz = A.bf16(8 * TQ).rearrange("p (k t) -> p k t", k=8)
    poolzb = K.buf("poolz")
    mark_p = A.lo
    CU = 347
    CQ = 342
    U0 = 128 - UH
    pooled2 = [A.bf16(2 * TQ).rearrange("p (k t) -> p k t", k=2) for _ in range(2)]
    pooled2b = [K.buf("pooled%d" % i) for i in range(2)]
    usb = [A.f32(TU + 7) for _ in range(2)]
    usbb = [K.buf("usb%d" % i) for i in range(2)]
    tmpa = [A.f32(TU + 7) for _ in range(2)]
    tmpab = [K.buf("tmpa%d" % i) for i in range(2)]

    def v3(ap):
        return ap.rearrange("p (c t) -> p c t", c=3)

    def pool_lin(gi4):
        pd, pdb = pooled2[gi4 % 2], pooled2b[gi4 % 2]
        Wl, Wlb = load_w(wpl, gi4 * 2, 2, 0, 256)
        for mj in range(2):
            mo = gi4 * 2 + mj
            gj = next_grp()
            Gl = grp(gj, 3, CQ)
            for k in range(2):
                for c in range(3):
                    mm(Gl[:, c, :], Wl[:, k, mj * 128:(mj + 1) * 128], pd[:, k, c * CQ:(c + 1) * CQ],
                       k == 0, k == 1, [Wlb, pdb], grpb[gj])
            act(v3(poolz[:, mo, :]), Gl, AF.Identity, reads=grpb[gj] + [tabb], pwrites=[poolzb],
                scale=tab[:, TB_PS + mo:TB_PS + mo + 1])

    lin_pending = None
    for blk in (1, 0):
        W, Wb = load_w(w_in, 0, 16, blk * 512, 512)
        for mi in (2, 3, 0, 1):
            m = blk * 4 + mi
            gi = next_grp()
            G = grp(gi, 3, CU)
            for k in range(KC):
                for c in range(3):
                    mm(G[:, c, :], W[:, k, mi * 128:(mi + 1) * 128], h1x[:, k, U0 + c * CU:U0 + (c + 1) * CU], k == 0, k == KC - 1,
                       [Wb, h1xb[k % 2]], grpb[gi])
            u = usb[m % 2]
            ub = usbb[m % 2]
            dve_tt(v3(u[:, 0:TU]), G, v3(rsf[:, U0:U0 + TU]), ALU.mult, reads=grpb[gi] + [rsfb], writes=[ub])
            if lin_pending is not None:
                pool_lin(lin_pending)
                lin_pending = None
            w = POOLW[m // 2]
            cur, curb = u, ub
            vf = 0
            span = 1
            ti = 0
            while span < w:
                nx, nxb = tmpa[ti % 2], tmpab[ti % 2]
                ti += 1
                dve_tt(nx[:, vf + span:TU], cur[:, vf + span:TU], cur[:, vf:TU - span], ALU.add, reads=[curb], writes=[nxb])
                cur, curb = nx, nxb
                vf += span
                span *= 2
            cs = TB_CORR + (m // 2) * 18
            dve_tt(cur[:, UH:UH + 18], cur[:, UH:UH + 18], tab[:, cs:cs + 18], ALU.mult, reads=[curb, tabb], pwrites=[curb])
            gi4 = m // 2
            pd, pdb = pooled2[gi4 % 2], pooled2b[gi4 % 2]
            dve_stt(pd[:, m % 2, :], cur[:, UH:TU], 1.0 / w, u[:, UH:TU], ALU.mult, ALU.subtract,
                    reads=[curb, ub], writes=[pdb] if m % 2 == 0 else (), pwrites=() if m % 2 == 0 else [pdb])
            if m % 2 == 1:
                lin_pending = gi4
    pool_lin(lin_pending)
    K.barrier()
    A.lo = mark_p

    sg0 = A.f32(TQ)
    sg1 = A.f32(TQ)
    t0 = A.f32(TQ)
    sg0b, sg1b, t0b = K.buf("sg0"), K.buf("sg1"), K.buf("t0")

    for blk in range(4):
        Wg0, Wg0b = load_w(w_in, 0, 16, OG + blk * 512, 512)
        Wg1, Wg1b = load_w(w_in, 0, 16, OG + D + blk * 512, 512)
        Wpo, Wao, Wpob = load_w2(blk)
        Waob = Wpob
        for mi in range(4):
            m = blk * 4 + mi
            msl = slice(mi * 128, (mi + 1) * 128)
            Msl = slice(m * 128, (m + 1) * 128)
            G = grp(0, 3, CQ)
            for k in range(KC):
                for c in range(3):
                    mm(G[:, c, :], Wg0[:, k, msl], h1x[:, k, 128 + c * CQ:128 + (c + 1) * CQ], k == 0, k == KC - 1,
                       [Wg0b, h1xb[k % 2]], grpb[0])
            dve_tt(v3(sg0[:, :]), G, v3(rsf[:, 128:128 + TQ]), ALU.mult, reads=grpb[0] + [rsfb], writes=[sg0b])
            act(sg0[:, :], sg0[:, :], AF.Sigmoid, reads=[sg0b, tabb], writes=[sg0b], bias=tab[:, TB_BG + m:TB_BG + m + 1], scale=1.0)
            G1 = grp(1, 3, CQ)
            for k in range(KC):
                for c in range(3):
                    mm(G1[:, c, :], Wg1[:, k, msl], h1x[:, k, 128 + c * CQ:128 + (c + 1) * CQ], k == 0, k == KC - 1,
                       [Wg1b, h1xb[k % 2]], grpb[1])
            dve_tt(v3(sg1[:, :]), G1, v3(rsf[:, 128:128 + TQ]), ALU.mult, reads=grpb[1] + [rsfb], writes=[sg1b])
            act(sg1[:, :], sg1[:, :], AF.Sigmoid, reads=[sg1b, tabb], writes=[sg1b],
                bias=tab[:, TB_BG + 16 + m:TB_BG + 16 + m + 1], scale=1.0)
            for k in range(8):
                for c in range(3):
                    mm(G[:, c, :], Wpo[:, k, msl], poolz[:, k, c * CQ:(c + 1) * CQ], k == 0, k == 7, [Wpob, poolzb], grpb[0])
            dve_tt(v3(t0[:, :]), G, v3(sg0[:, :]), ALU.mult, reads=grpb[0] + [sg0b], writes=[t0b])
            for k in range(4):
                for c in range(3):
                    mm(G1[:, c, :], Wao[:, k, msl], attnT[:, k, c * CQ:(c + 1) * CQ], k == 0, k == 3, [Waob, attnTb], grpb[1])
            dve_tt(v3(sg1[:, :]), G1, v3(sg1[:, :]), ALU.mult, reads=grpb[1] + [sg1b], writes=[sg1b])
            dve_tt(mixedT[:, m, :], t0[:, :], sg1[:, :], ALU.add, reads=[t0b, sg1b], pwrites=[mixedb])
    K.barrier()
    A.lo = mark_m
    A.hi = ARENA_WORDS

    x1T = A.f32(16 * TQ, top=True).rearrange("p (k t) -> p k t", k=16)
    x1b = [K.buf("x1_%d" % m) for m in range(16)]
    xr = [A.f32(TQ) for _ in range(2)]
    xrb = [K.buf("xr%d" % i, dma=True) for i in range(2)]
    for blk in range(4):
        W, Wb = load_w(w_out, 0, 16, blk * 512, 512)
        for mi in range(4):
            m = blk * 4 + mi
            gi = next_grp()
            G = grp(gi, 3, CQ)
            dma("sp", xr[m % 2], xT[m * 128:(m + 1) * 128, J0:J0 + TQ], xrb[m % 2], writes=[xrb[m % 2]])
            for k in range(KC):
                for c in range(3):
                    mm(G[:, c, :], W[:, k, mi * 128:(mi + 1) * 128], mixedT[:, k, c * CQ:(c + 1) * CQ], k == 0, k == KC - 1,
                       [Wb, mixedb], grpb[gi])
            dve_tt(v3(x1T[:, m, :]), G, v3(xr[m % 2][:, :]), ALU.add, reads=grpb[gi] + [xrb[m % 2]], writes=[x1b[m]])
    K.barrier()
    A.lo = mark_a

    h2T = A.bf16(16 * TQ).rearrange("p (k t) -> p k t", k=16)
    h2b = (K.buf("h2e"), K.buf("h2o"))
    NQF = 11
    prodT_flat = A.bf16(NQF * T)
    prodT = prodT_flat.rearrange("p (k t) -> p k t", k=NQF)
    prodb = K.buf("prodT")
    rs2 = rsblk[:, 0:TQ]
    rs2b = K.buf("rs2")
    sq2 = [prodT_flat[:, i * 4 * CQ:(i + 1) * 4 * CQ].rearrange("p (k t) -> p k t", k=4) for i in range(2)]
    sq2b = [K.buf("sq2_%d" % i) for i in range(2)]
    last_sq_read = None
    for c in range(3):
        cols = slice(c * CQ, (c + 1) * CQ)
        b6 = 6 + st["b6"]
        st["b6"] = 1 - st["b6"]
        pb = bank(b6)
        for q4 in range(4):
            i = (c * 4 + q4) % 2
            act(sq2[i], x1T[:, 4 * q4:4 * q4 + 4, cols], AF.Square, reads=x1b[4 * q4:4 * q4 + 4], writes=[sq2b[i]])
            for kk in range(4):
                k = 4 * q4 + kk
                last_sq_read = mm(pb[:, 0:CQ], ones[:, :], sq2[i][:, kk, :], k == 0, k == KC - 1, [sq2b[i], onesb], [bk[b6]])
        act(rs2[:, cols], pb[:, 0:CQ], AF.Sqrt, reads=[bk[b6], epsb], pwrites=[rs2b], bias=epsb_ap[:, 0:1], scale=1.0 / D)
        K.op("dve", lambda e, o=rs2[:, cols]: e.reciprocal(out=o, in_=o), reads=[rs2b], pwrites=[rs2b])
        for k in range(KC):
            gk = tab[:, TB_GFFN + k:TB_GFFN + k + 1]
            K.op("dve", lambda e, o=h2T[:, k, cols], i_=x1T[:, k, cols], s_=gk: e.tensor_scalar_mul(out=o, in0=i_, scalar1=s_),
                 reads=[x1b[k], tabb], pwrites=[h2b[k % 2]])

    ga = A.f32(T).rearrange("p (k t) -> p k t", k=1)
    gab = [K.buf("ga0"), K.buf("ga1")]
    preS = [A.f32(TQ)] * 2
    preSb = [K.buf("preS0")] * 2
    c0t = [A.f32(TQ) for _ in range(2)]
    c0b = [K.buf("c0_%d" % i) for i in range(2)]
    cvn = 0

    def conv_chunk(W, Wb, ci, f):
        nonlocal cvn
        gi = next_grp()
        G = grp(gi, 3, CQ)
        for k in range(KC):
            for c in range(3):
                mm(G[:, c, :], W[:, k, ci * 128:(ci + 1) * 128], h2T[:, k, c * CQ:(c + 1) * CQ], k == 0, k == KC - 1,
                   [Wb, h2b[k % 2]], grpb[gi])
        p = cvn % 2
        cvn += 1
        ps_, psb_ = preS[p], preSb[p]
        cc, ccb = c0t[p], c0b[p]
        dve_tt(v3(ps_[:, :]), G, v3(rs2), ALU.mult, reads=grpb[gi] + [rs2b], writes=[psb_])
        act(cc[:, :], ps_[:, :], AF.Identity, reads=[psb_, tabb], writes=[ccb],
            bias=tab[:, TB_CB + f:TB_CB + f + 1], scale=tab[:, TB_CW + 2 * 88 + f:TB_CW + 2 * 88 + f + 1])
        dve_stt(cc[:, 1:TQ], ps_[:, 0:TQ - 1], tab[:, TB_CW + 88 + f:TB_CW + 88 + f + 1], cc[:, 1:TQ], ALU.mult, ALU.add,
                reads=[psb_, ccb, tabb], writes=[ccb])
        dve_stt(cc[:, 2:TQ], ps_[:, 0:TQ - 2], tab[:, TB_CW + f:TB_CW + f + 1], cc[:, 2:TQ], ALU.mult, ALU.add,
                reads=[psb_, ccb, tabb], writes=[ccb])
        return cc, ccb

    def load_wab(f):
        dst, wb_ = load_flat(w_up[f * 128:(f + 1) * 128, :], 4096)
        return (dst[:, 0:2048].rearrange("p (k n) -> p k n", k=16),
                dst[:, 2048:4096].rearrange("p (k n) -> p k n", k=16), wb_)

    units = []
    for qq in range(4):
        for f in range(qq * NQF, (qq + 1) * NQF):
            units.append(("A", f))
            units.append(("B", f))
        units.append(("D", qq))
    early = set()
    for i in range(len(units) - 1):
        if units[i][0] == "D" and units[i + 1][0] == "A":
            units[i], units[i + 1] = units[i + 1], units[i]
            early.add(units[i][1])
    wabs = {}
    for kind, v in units:
        if kind == "A":
            f = v
            if f in early:
                Wa, Wab = load_w(w_up, 0, 16, f * 128, 128)
            else:
                wabs[f] = load_wab(f)
                Wa, Wbk, Wab = wabs[f]
            cc, ccb = conv_chunk(Wa, Wab, 0, f)
            act(ga[:, 0, :], cc[:, 2:TQ], AF.Gelu, reads=[ccb], writes=[gab[0]])
        elif kind == "B":
            f = v
            if f in early:
                Wbk, Wab = load_w(w_up, 0, 16, DFF + f * 128, 128)
            else:
                Wa, Wbk, Wab = wabs.pop(f)
            f0 = (f // NQF) * NQF
            cc, ccb = conv_chunk(Wbk, Wab, 0, FC + f)
            dve_tt(prodT[:, f - f0, :], cc[:, 2:TQ], ga[:, 0, :], ALU.mult, reads=[ccb, gab[0]], pwrites=[prodb],
                   extra=[last_sq_read] if f == 0 else ())
        else:
            qq = v
            f0 = qq * NQF
            for blk in range(4):
                Wd, Wdb = load_w(w_down, f0, NQF, blk * 512, 512)
                for mi in range(4):
                    m = blk * 4 + mi
                    gi = next_grp()
                    G = grp(gi, 2, 512)
                    for k in range(NQF):
                        for c in range(2):
                            mm(G[:, c, :], Wd[:, k, mi * 128:(mi + 1) * 128], prodT[:, k, c * 512:(c + 1) * 512],
                               k == 0, k == NQF - 1, [Wdb, prodb], grpb[gi])
                    xv = x1T[:, m, 2:TQ].rearrange("p (c t) -> p c t", c=2)
                    dve_tt(xv, G, xv, ALU.add, reads=grpb[gi] + [x1b[m]], writes=[x1b[m]])
    K.barrier()
    A.lo = mark_a

    sqf = A.bf16(16 * 256).rearrange("p (k t) -> p k t", k=16)
    sqfb = K.buf("sqf")
    ost = [A.f32(16 * 256).rearrange("p (k t) -> p k t", k=16) for _ in range(2)]
    ostb = [K.buf("ost%d" % i, dma=True) for i in range(2)]
    for c in range(4):
        xv = x1T[:, :, 2 + c * 256:2 + (c + 1) * 256]
        norm_core(xv, x1b, 256, TB_GF, ost[c % 2], ostb[c % 2], sqf, sqfb)
        dma("sp", outT.rearrange("(k p) t -> p k t", p=128)[:, :, c * 256:(c + 1) * 256], ost[c % 2], ostb[c % 2],
            reads=[ostb[c % 2]])
    K.barrier()
    K.op("sp", None)
    K.emit(nc)
    return nc, kvcols


_CACHE = {}


def _host_tables(inputs, kvcols):
    def colmaj(v, nchunk):
        return np.ascontiguousarray(np.asarray(v, np.float32).reshape(nchunk, 128).T)

    base = np.zeros((128, NTAB), np.float32)
    base[:, TB_GMIX:TB_GMIX + 16] = colmaj(inputs["g_mix"][0], 16)
    base[:, TB_BG:TB_BG + 32] = colmaj(inputs["b_gate"][0], 32)
    base[:, TB_PS:TB_PS + 8] = colmaj(inputs["pool_scale"][0], 8)
    base[:, TB_GFFN:TB_GFFN + 16] = colmaj(inputs["g_ffn"][0], 16)
    cw = np.asarray(inputs["conv_w"][0], np.float32)
    for i in range(3):
        base[:, TB_CW + i * 88:TB_CW + (i + 1) * 88] = colmaj(cw[i], 88)
    base[:, TB_CB:TB_CB + 88] = colmaj(inputs["conv_b"][0], 88)
    base[:, TB_GF:TB_GF + 16] = colmaj(inputs["g_final"], 16)
    tabs = []
    for c in range(NCORES):
        t = base.copy()
        s0 = c * T
        for wi, w in enumerate(POOLW):
            for col in range(18):
                tt = s0 - 2 + col
                t[:, TB_CORR + wi * 18 + col] = (w / min(tt + 1, w)) if tt >= 0 else 1.0
        for (g, start), (idx, n) in kvcols.items():
            d = DIL[g]
            i = np.arange(128)
            tt = s0 - 2 - J0 + (J0 - HALO[g]) + start + d * i
            t[:, TB_KV + idx] = np.where(tt >= 0, 0.0, NEG)
        tabs.append(t)
    masks = np.zeros((128, 12, 256), np.float32)
    kk = np.arange(128)[:, None]
    ii = np.arange(128)[None, :]
    for hh in range(12):
        g = hh // 4
        d = DIL[g]
        slope = np.float32(2.0 ** (-8.0 * (hh + 1) / 12))
        jp = ii + 128 - kk
        masks[:, hh, 0:128] = np.where(kk >= ii, -slope * (jp * d).astype(np.float32), NEG)
        jc = ii - kk
        masks[:, hh, 128:256] = np.where(kk <= ii, -slope * (jc * d).astype(np.float32), NEG)
    return tabs, masks.reshape(128, 12 * 256)


def kernel(**inputs):
    if "prog" not in _CACHE:
        _CACHE["prog"] = build_program()
    nc, kvcols = _CACHE["prog"]
    x = np.asarray(inputs["x"], np.float32)[0]
    S = x.shape[0]
    xTfull = np.zeros((D, J0 + 2 + S), np.float32)
    xTfull[:, J0 + 2:] = x.T
    tabs, masks = _host_tables(inputs, kvcols)
    ident = np.eye(128, dtype=np.float32)
    def tile_blocks(W, nk, ncols):
        W = np.asarray(W, np.float32)
        nb = W.shape[1] // ncols
        return np.ascontiguousarray(W.reshape(nk, 128, nb, ncols).transpose(2, 1, 0, 3)).reshape(nb * 128, nk * ncols)

    w_up_full = np.asarray(inputs["w_up"], np.float32)[0]
    w_down_full = np.asarray(inputs["w_down"], np.float32)[0]
    wplf = np.asarray(inputs["w_pool_lin"], np.float32)[0]
    shared = {
        "w_in": tile_blocks(np.asarray(inputs["w_in"], np.float32)[0], 16, 512),
        "wpl": np.concatenate([tile_blocks(wplf[gi], 2, 256) for gi in range(4)], axis=0),
        "wpa": np.concatenate([tile_blocks(np.asarray(inputs["w_pool_out"], np.float32)[0], 8, 512),
                               tile_blocks(np.asarray(inputs["w_attn_out"], np.float32)[0], 4, 512)], axis=1),
        "w_out": tile_blocks(np.asarray(inputs["w_out"], np.float32)[0], 16, 512),
        "w_up": np.concatenate([tile_blocks(w_up_full[:, :DFF], 16, 128), tile_blocks(w_up_full[:, DFF:], 16, 128)], axis=1),
        "w_down": np.concatenate([tile_blocks(w_down_full[q * 1408:(q + 1) * 1408], 11, 512) for q in range(4)], axis=0),
        "masks": masks,
        "ident": ident,
    }
    in_maps = []
    for c in range(NCORES):
        m = dict(shared)
        m["xT"] = np.ascontiguousarray(xTfull[:, c * T:c * T + NT])
        m["tab"] = tabs[c]
        in_maps.append(m)
    res = run_bass_kernel_spmd(nc, in_maps, core_ids=list(range(NCORES)))
    out = np.empty((1, S, D), np.float32)
    for c in range(NCORES):
        out[0, c * T:(c + 1) * T, :] = np.asarray(res.results[c]["outT"]).T
    return out
```
